# Optimizing a Trainium2 kernel written in Bass

```python
import jax, jax.numpy as jnp
from jax import lax
import numpy as np

D_MODEL = 1024
BATCH = 16
SEQ = 4096
DEPTH = 1
DEC_BATCH = 32
DEC_SEQ = 16
PAST_LEN = 4096

CHUNK = 64
D_MIX = D_MODEL
H_A = 4
W_A = D_MIX // 2
DH_A = W_A // H_A
H_B = 4
W_B = D_MIX - W_A
DH_B = W_B // H_B
CONV_W = 4
N_MEM = 256
N_XHEADS = 4
DH_X = D_MODEL // N_XHEADS
D_FF = 4 * D_MODEL
IN_COLS = 4 * W_A + 2 * H_A + 4 * W_B + 2 * H_B
EPS = 1e-6

kernel_name = 'hybrid_mlstm_gdn_streaming_step'


def rms_norm(x, w):
    xf = x.astype(jnp.float32)
    y = xf * lax.rsqrt(jnp.mean(xf * xf, axis=-1, keepdims=True) + EPS)
    return (y * w.astype(jnp.float32)).astype(x.dtype)


def head_rms_norm(h, w):
    return h * lax.rsqrt(jnp.mean(h * h, axis=-1, keepdims=True) + EPS) * w.astype(jnp.float32)


def l2_normalize(x):
    return x * lax.rsqrt(jnp.sum(x * x, axis=-1, keepdims=True) + EPS)


def split_heads(a, n_heads):
    b, t, _ = a.shape
    return a.reshape(b, t, n_heads, -1).transpose(0, 2, 1, 3)


def merge_heads(a):
    b, h, t, dh = a.shape
    return a.transpose(0, 2, 1, 3).reshape(b, t, h * dh)


def to_chunks(a, chunk):
    b, h, t = a.shape[:3]
    a = a.reshape((b, h, t // chunk, chunk) + a.shape[3:])
    return jnp.moveaxis(a, 2, 0)


def from_chunks(a):
    a = jnp.moveaxis(a, 0, 2)
    b, h, nc, l = a.shape[:4]
    return a.reshape((b, h, nc * l) + a.shape[4:])


def mlstm_chunk_step(carry, inp):
    C0, n0, m0 = carry
    q, k, v, ig, lf = inp
    L = q.shape[-2]
    causal = jnp.tril(jnp.ones((L, L), dtype=bool))
    b = jnp.cumsum(lf, axis=-1)
    d = b[..., :, None] - b[..., None, :] + ig[..., None, :]
    d = jnp.where(causal, d, -jnp.inf)
    inter = b + m0[..., None]
    m = jnp.maximum(inter, jnp.max(d, axis=-1))
    w_intra = jnp.exp(d - m[..., None])
    w_inter = jnp.exp(inter - m)
    s = jnp.einsum('bhtd,bhsd->bhts', q, k) * w_intra
    num = w_inter[..., None] * jnp.einsum('bhtk,bhkv->bhtv', q, C0) + jnp.einsum('bhts,bhsv->bhtv', s, v)
    den = w_inter * jnp.einsum('bhtk,bhk->bht', q, n0) + jnp.sum(s, axis=-1)
    h = num / jnp.maximum(jnp.abs(den), jnp.exp(-m))[..., None]
    b_last = b[..., -1]
    m_last = m[..., -1]
    w_s = jnp.exp(b_last[..., None] - b + ig - m_last[..., None])
    decay0 = jnp.exp(b_last + m0 - m_last)
    C1 = decay0[..., None, None] * C0 + jnp.einsum('bhs,bhsk,bhsv->bhkv', w_s, k, v)
    n1 = decay0[..., None] * n0 + jnp.einsum('bhs,bhsk->bhk', w_s, k)
    return (C1, n1, m_last), h


def gdn_chunk_step(S0, inp):
    q, k, v, g, beta = inp
    L = q.shape[-2]
    incl = jnp.tril(jnp.ones((L, L), dtype=bool))
    strict = jnp.tril(jnp.ones((L, L), dtype=bool), -1)
    G = jnp.cumsum(g, axis=-1)
    decay = jnp.exp(jnp.where(incl, G[..., :, None] - G[..., None, :], -jnp.inf))
    kk = jnp.einsum('bhtd,bhsd->bhts', k, k)
    A = jnp.where(strict, beta[..., None] * decay * kk, 0.0)
    gam = jnp.exp(G)
    rhs = beta[..., None] * (v - gam[..., None] * jnp.einsum('bhtk,bhkv->bhtv', k, S0))
    W = lax.linalg.triangular_solve(A + jnp.eye(L, dtype=A.dtype), rhs,
                                    left_side=True, lower=True, unit_diagonal=True)
    qk = jnp.einsum('bhtd,bhsd->bhts', q, k) * decay
    o = gam[..., None] * jnp.einsum('bhtk,bhkv->bhtv', q, S0) + jnp.einsum('bhts,bhsv->bhtv', qk, W)
    G_last = G[..., -1]
    S1 = jnp.exp(G_last)[..., None, None] * S0 + jnp.einsum('bhs,bhsk,bhsv->bhkv', jnp.exp(G_last[..., None] - G), k, W)
    return S1, o


def causal_short_conv(u, buf, w):
    t = u.shape[1]
    up = jnp.concatenate([buf.astype(u.dtype), u], axis=1)
    y = up[:, 0:t] * w[0]
    for j in range(1, CONV_W):
        y = y + up[:, j:j + t] * w[j]
    return y, up[:, up.shape[1] - (CONV_W - 1):]


def hybrid_mixer(xn, C0, n0, m0, S0, conv_buf, chunk, w_in, mlstm_igate_b, mlstm_fgate_b,
                 mlstm_norm_w, gdn_conv_w, gdn_A_log, gdn_dt_bias, gdn_norm_w, w_out):
    f32 = jnp.float32
    proj = xn @ w_in
    sizes = [W_A, W_A, W_A, W_A, H_A, H_A, 3 * W_B, W_B, H_B, H_B]
    idx = np.cumsum(sizes)[:-1].tolist()
    qa, ka, va, oa, ia, fa, qkvb, zb, ab, bb = jnp.split(proj, idx, axis=-1)
    q = split_heads(qa.astype(f32), H_A)
    k = split_heads(ka.astype(f32), H_A) * (DH_A ** -0.5)
    v = split_heads(va.astype(f32), H_A)
    ig = (ia.astype(f32) + mlstm_igate_b.astype(f32)).transpose(0, 2, 1)
    lf = jax.nn.log_sigmoid(fa.astype(f32) + mlstm_fgate_b.astype(f32)).transpose(0, 2, 1)
    carry0 = (C0.astype(f32), n0.astype(f32), m0.astype(f32))
    (C1, n1, m1), h = lax.scan(mlstm_chunk_step, carry0,
                               (to_chunks(q, chunk), to_chunks(k, chunk), to_chunks(v, chunk),
                                to_chunks(ig, chunk), to_chunks(lf, chunk)))
    h = from_chunks(h)
    h_a = jax.nn.sigmoid(oa.astype(f32)) * merge_heads(head_rms_norm(h, mlstm_norm_w))
    c_out, conv_new = causal_short_conv(qkvb, conv_buf, gdn_conv_w)
    c_out = jax.nn.silu(c_out.astype(f32))
    qb, kb, vb = jnp.split(c_out, 3, axis=-1)
    qb = l2_normalize(split_heads(qb, H_B)) * (DH_B ** -0.5)
    kb = l2_normalize(split_heads(kb, H_B))
    vb = split_heads(vb, H_B)
    g = -jnp.exp(gdn_A_log.astype(f32)) * jax.nn.softplus(ab.astype(f32) + gdn_dt_bias.astype(f32))
    beta = jax.nn.sigmoid(bb.astype(f32))
    g = g.transpose(0, 2, 1)
    beta = beta.transpose(0, 2, 1)
    S1, o = lax.scan(gdn_chunk_step, S0.astype(f32),
                     (to_chunks(qb, chunk), to_chunks(kb, chunk), to_chunks(vb, chunk),
                      to_chunks(g, chunk), to_chunks(beta, chunk)))
    o = from_chunks(o)
    h_b = merge_heads(head_rms_norm(o, gdn_norm_w)) * jax.nn.silu(zb.astype(f32))
    y = jnp.concatenate([h_a, h_b], axis=-1).astype(xn.dtype) @ w_out
    return y, C1, n1, m1, S1, conv_new


def memory_kv(mem, norm_mem_w, wk, wv):
    b = mem.shape[0]
    mn = rms_norm(mem, norm_mem_w)
    mk = (mn @ wk).reshape(b, N_MEM, N_XHEADS, DH_X)
    mv = (mn @ wv).reshape(b, N_MEM, N_XHEADS, DH_X)
    return mk, mv


def memory_cross_attention(xn, mem_k, mem_v, wq, wo):
    b, t, _ = xn.shape
    q = (xn @ wq).reshape(b, t, N_XHEADS, DH_X)
    s = jnp.einsum('bthd,bnhd->bhtn', q, mem_k.astype(q.dtype)).astype(jnp.float32) * (DH_X ** -0.5)
    p = jax.nn.softmax(s, axis=-1).astype(xn.dtype)
    o = jnp.einsum('bhtn,bnhd->bthd', p, mem_v.astype(xn.dtype)).reshape(b, t, D_MODEL)
    return o @ wo


def squared_relu_mlp(xn, w1, w2):
    hdn = jax.nn.relu(xn @ w1)
    return (hdn * hdn) @ w2


def trunk_layer(x, mem_k, mem_v, C0, n0, m0, S0, conv_buf, chunk, norm_mix_w, w_in, mlstm_igate_b,
                mlstm_fgate_b, mlstm_norm_w, gdn_conv_w, gdn_A_log, gdn_dt_bias, gdn_norm_w, w_out,
                norm_x_w, wq_x, wo_x, norm_ffn_w, w_ff1, w_ff2):
    y, C1, n1, m1, S1, conv_new = hybrid_mixer(
        rms_norm(x, norm_mix_w), C0, n0, m0, S0, conv_buf, chunk, w_in, mlstm_igate_b, mlstm_fgate_b,
        mlstm_norm_w, gdn_conv_w, gdn_A_log, gdn_dt_bias, gdn_norm_w, w_out)
    x = x + y
    x = x + memory_cross_attention(rms_norm(x, norm_x_w), mem_k, mem_v, wq_x, wo_x)
    x = x + squared_relu_mlp(rms_norm(x, norm_ffn_w), w_ff1, w_ff2)
    return x, C1.astype(x.dtype), n1.astype(x.dtype), m1.astype(x.dtype), S1.astype(x.dtype), conv_new


def setup_inputs(seed: int = 0) -> dict:
    key = jax.random.key(seed)
    keys = list(jax.random.split(key, 48))
    f32 = jnp.float32

    def nrm(shape, scale):
        return jax.random.normal(keys.pop(), shape, f32) * scale

    def gain(shape):
        return 1.0 + 0.1 * jax.random.normal(keys.pop(), shape, f32)

    dt = jnp.exp(jax.random.uniform(keys.pop(), (DEPTH, H_B), f32, np.log(1e-3), np.log(1e-1)))
    inputs = {
        'x_prompt': nrm((BATCH, SEQ, D_MODEL), 1.0),
        'x_sample': nrm((DEC_BATCH, DEC_SEQ, D_MODEL), 1.0),
        'state_mlstm_C': nrm((DEPTH, DEC_BATCH, H_A, DH_A, DH_A), DH_A ** -0.5),
        'state_mlstm_n': nrm((DEPTH, DEC_BATCH, H_A, DH_A), DH_A ** -0.5),
        'state_mlstm_m': nrm((DEPTH, DEC_BATCH, H_A), 1.0),
        'state_gdn_S': nrm((DEPTH, DEC_BATCH, H_B, DH_B, DH_B), DH_B ** -0.5),
        'state_gdn_conv': nrm((DEPTH, DEC_BATCH, CONV_W - 1, 3 * W_B), 1.0),
        'cache_mem_k': nrm((DEPTH, DEC_BATCH, N_MEM, N_XHEADS, DH_X), 1.0),
        'cache_mem_v': nrm((DEPTH, DEC_BATCH, N_MEM, N_XHEADS, DH_X), 1.0),
        'mem_prompt': nrm((BATCH, N_MEM, D_MODEL), 1.0),
        'norm_mix_w': gain((DEPTH, D_MODEL)),
        'w_in': nrm((DEPTH, D_MODEL, IN_COLS), D_MODEL ** -0.5),
        'mlstm_igate_b': nrm((DEPTH, H_A), 0.1),
        'mlstm_fgate_b': jnp.linspace(3.0, 6.0, H_A, dtype=f32)[None, :] + nrm((DEPTH, H_A), 0.1),
        'mlstm_norm_w': gain((DEPTH, DH_A)),
        'gdn_conv_w': nrm((DEPTH, CONV_W, 3 * W_B), CONV_W ** -0.5),
        'gdn_A_log': jnp.log(jax.random.uniform(keys.pop(), (DEPTH, H_B), f32, 1.0, 16.0)),
        'gdn_dt_bias': dt + jnp.log(-jnp.expm1(-dt)),
        'gdn_norm_w': gain((DEPTH, DH_B)),
        'w_out': nrm((DEPTH, D_MIX, D_MODEL), D_MIX ** -0.5),
        'norm_x_w': gain((DEPTH, D_MODEL)),
        'norm_mem_w': gain((DEPTH, D_MODEL)),
        'wq_x': nrm((DEPTH, D_MODEL, D_MODEL), D_MODEL ** -0.5),
        'wk_x': nrm((DEPTH, D_MODEL, D_MODEL), D_MODEL ** -0.5),
        'wv_x': nrm((DEPTH, D_MODEL, D_MODEL), D_MODEL ** -0.5),
        'wo_x': nrm((DEPTH, D_MODEL, D_MODEL), D_MODEL ** -0.5),
        'norm_ffn_w': gain((DEPTH, D_MODEL)),
        'w_ff1': nrm((DEPTH, D_MODEL, D_FF), D_MODEL ** -0.5),
        'w_ff2': nrm((DEPTH, D_FF, D_MODEL), D_FF ** -0.5),
        'norm_final_w': gain((D_MODEL,)),
    }
    return inputs


def reference(x_prompt, x_sample, state_mlstm_C, state_mlstm_n, state_mlstm_m, state_gdn_S,
              state_gdn_conv, cache_mem_k, cache_mem_v, mem_prompt, norm_mix_w, w_in, mlstm_igate_b,
              mlstm_fgate_b, mlstm_norm_w, gdn_conv_w, gdn_A_log, gdn_dt_bias, gdn_norm_w, w_out,
              norm_x_w, norm_mem_w, wq_x, wk_x, wv_x, wo_x, norm_ffn_w, w_ff1, w_ff2, norm_final_w):
    f32 = jnp.float32
    bp = x_prompt.shape[0]
    sample_chunk = x_sample.shape[1]
    hp, hs = x_prompt, x_sample
    pC, pn, pm, pS, pconv, pmk, pmv = [], [], [], [], [], [], []
    sC, sn, sm, sS, sconv = [], [], [], [], []
    for l in range(DEPTH):
        mk_p, mv_p = memory_kv(mem_prompt, norm_mem_w[l], wk_x[l], wv_x[l])
        hp, C1, n1, m1, S1, cv1 = trunk_layer(
            hp, mk_p, mv_p,
            jnp.zeros((bp, H_A, DH_A, DH_A), f32), jnp.zeros((bp, H_A, DH_A), f32),
            jnp.zeros((bp, H_A), f32), jnp.zeros((bp, H_B, DH_B, DH_B), f32),
            jnp.zeros((bp, CONV_W - 1, 3 * W_B), hp.dtype), CHUNK,
            norm_mix_w[l], w_in[l], mlstm_igate_b[l], mlstm_fgate_b[l], mlstm_norm_w[l], gdn_conv_w[l],
            gdn_A_log[l], gdn_dt_bias[l], gdn_norm_w[l], w_out[l], norm_x_w[l], wq_x[l], wo_x[l],
            norm_ffn_w[l], w_ff1[l], w_ff2[l])
        pC.append(C1); pn.append(n1); pm.append(m1); pS.append(S1); pconv.append(cv1)
        pmk.append(mk_p); pmv.append(mv_p)
        hs, C2, n2, m2, S2, cv2 = trunk_layer(
            hs, cache_mem_k[l], cache_mem_v[l], state_mlstm_C[l], state_mlstm_n[l], state_mlstm_m[l],
            state_gdn_S[l], state_gdn_conv[l], sample_chunk,
            norm_mix_w[l], w_in[l], mlstm_igate_b[l], mlstm_fgate_b[l], mlstm_norm_w[l], gdn_conv_w[l],
            gdn_A_log[l], gdn_dt_bias[l], gdn_norm_w[l], w_out[l], norm_x_w[l], wq_x[l], wo_x[l],
            norm_ffn_w[l], w_ff1[l], w_ff2[l])
        sC.append(C2); sn.append(n2); sm.append(m2); sS.append(S2); sconv.append(cv2)
    y_prompt = rms_norm(hp, norm_final_w)
    y_sample = rms_norm(hs, norm_final_w)
    return (y_prompt, y_sample,
            jnp.stack(pC), jnp.stack(pn), jnp.stack(pm), jnp.stack(pS), jnp.stack(pconv),
            jnp.stack(pmk), jnp.stack(pmv),
            jnp.stack(sC), jnp.stack(sn), jnp.stack(sm), jnp.stack(sS), jnp.stack(sconv))
```

```python
import os
import numpy as np
from contextlib import ExitStack
import concourse.bass as bass
import concourse.mybir as mybir
from concourse.bass_utils import run_bass_kernel_spmd

F32 = mybir.dt.float32
BF16 = mybir.dt.bfloat16
F32R = mybir.dt.float32r
ALU = mybir.AluOpType
AF = mybir.ActivationFunctionType
AX = mybir.AxisListType

NCORES = 8
D = 1024
SEQ = 4096
NPS = 2
NSS = 4
LS = 16
NMEM = 256
EPS = 1e-6
NSLOT = 3
ND = 24
NF32 = int(os.environ.get("KNF32", "99"))


class Dep:
    __slots__ = ("w", "r", "excl", "wreal")

    def __init__(self, excl=False):
        self.w = None
        self.r = {}
        self.excl = excl
        self.wreal = True


class Eng:
    def __init__(self, name, h, sem):
        self.name = name
        self.h = h
        self.sem = sem
        self.cnt = 0
        self.seen = {}


class KB:
    def __init__(self, stage):
        self.stage = stage
        self.nc = bass.Bass("TRN2", target_bir_lowering=False)
        self.es = ExitStack()
        self.sem_owner = {}
        self.nwait = 0
        self.nins = 0

    def sb(self, name, shape, dt=F32):
        return self.es.enter_context(self.nc.sbuf_tensor(name, shape, dt))

    def psum(self, name, shape, dt=F32):
        return self.es.enter_context(self.nc.psum_tensor(name, shape, dt))

    def sem(self, name):
        return self.es.enter_context(self.nc.semaphore(name))

    def din(self, name, shape, dt=F32):
        return self.nc.dram_tensor(name, list(shape), dt, kind="ExternalInput").ap()

    def dout(self, name, shape, dt=F32):
        return self.nc.dram_tensor(name, list(shape), dt, kind="ExternalOutput").ap()

    def dint(self, name, shape, dt=F32):
        return self.nc.dram_tensor(name, list(shape), dt, kind="Internal").ap()

    def setup_engines(self):
        nc = self.nc
        self.PE = Eng("pe", nc.tensor, self.sem("s_pe"))
        self.ACT = Eng("act", nc.scalar, self.sem("s_act"))
        self.DVE = Eng("dve", nc.vector, self.sem("s_dve"))
        self.POOL = Eng("pool", nc.gpsimd, self.sem("s_pool"))
        self.SP = Eng("sp", nc.sync, self.sem("s_sp"))
        self.engs = [self.PE, self.ACT, self.DVE, self.POOL]
        for e in self.engs + [self.SP]:
            self.sem_owner[id(e.sem)] = e
        self.dsem = [self.sem("s_d%d" % i) for i in range(ND)]
        self.dcnt = [0] * ND
        self.dma_i = 0
        self.wsem = [self.sem("s_w%d" % i) for i in range(16)]
        self.wcnt = [0] * 16
        self.wma_i = 0

    def wait(self, E, tk, raw=False):
        sem, val = tk
        if sem is E.sem:
            if E is self.PE or not raw:
                return
        own = self.sem_owner.get(id(sem))
        if own is not None:
            assert val <= own.cnt, "waiting on a ticket not yet emitted (%s on %s %d>%d)" % (E.name, own.name, val, own.cnt)
        if E.seen.get(id(sem), 0) >= val:
            return
        E.h.wait_ge(sem, val)
        E.seen[id(sem)] = val
        self.nwait += 1

    def _pre(self, E, reads, writes):
        for d in reads:
            if d.w is not None:
                self.wait(E, d.w, raw=(d.wreal if d.excl else True))
        for d in writes:
            if d.w is not None:
                self.wait(E, d.w, raw=True)
            for tk in d.r.values():
                self.wait(E, tk)

    def _post(self, tk, reads, writes):
        for d in reads:
            if d.excl:
                d.w = tk
                d.wreal = False
                d.r = {}
                continue
            old = d.r.get(id(tk[0]))
            if old is None or old[1] < tk[1]:
                d.r[id(tk[0])] = tk
        for d in writes:
            d.w = tk
            d.wreal = True
            d.r = {}

    def op(self, E, fn, reads=(), writes=(), inc=True):
        self._pre(E, reads, writes)
        ins = fn()
        self.nins += 1
        if inc:
            E.cnt += 1
            ins.then_inc(E.sem, 1)
            tk = (E.sem, E.cnt)
        else:
            tk = (E.sem, E.cnt + 1)
        self._post(tk, reads, writes)
        return tk

    def dma(self, Q, out, in_, reads=(), writes=(), **kw):
        self._pre(Q, reads, writes)
        if Q is self.POOL:
            sems, cnts = self.wsem, self.wcnt
            slot = self.wma_i % 16
            self.wma_i += 1
        else:
            sems, cnts = self.dsem, self.dcnt
            slot = self.dma_i % ND
            self.dma_i += 1
        sem = sems[slot]
        if cnts[slot] > 0:
            self.wait(Q, (sem, cnts[slot]))
        cnts[slot] += 16
        Q.h.dma_start(out=out, in_=in_, **kw).then_inc(sem, 16)
        tk = (sem, cnts[slot])
        self._post(tk, reads, writes)
        return tk

    def barrier(self):
        for E in self.engs:
            for Fe in self.engs:
                if Fe is not E and Fe.cnt > 0:
                    self.wait(E, (Fe.sem, Fe.cnt))

    def mm(self, out, lhsT, rhs, start, stop, reads, writes, inc=None):
        if inc is None:
            inc = stop
        nc = self.nc
        return self.op(self.PE, lambda: nc.tensor.matmul(out, lhsT=lhsT, rhs=rhs, start=start, stop=stop),
                       reads, writes, inc=inc)

    def tr(self, out, in_, ident, reads, writes, inc=True):
        nc = self.nc
        return self.op(self.PE, lambda: nc.tensor.transpose(out, in_, ident), reads, writes, inc=inc)

    def act(self, out, in_, func, reads, writes, bias=None, scale=None, accum=None):
        nc = self.nc
        kw = {}
        if bias is not None:
            kw["bias"] = bias
        if scale is not None:
            kw["scale"] = scale
        if accum is not None:
            kw["accum_out"] = accum
        return self.op(self.ACT, lambda: nc.scalar.activation(out=out, in_=in_, func=func, **kw), reads, writes)

    def vec(self, E, name, reads, writes, **kw):
        h = E.h
        return self.op(E, lambda: getattr(h, name)(**kw), reads, writes)


def build(stage):
    k = KB(stage)
    nc = k.nc
    xp = k.din("xp", [NPS, SEQ, D])
    xs = k.din("xs", [NSS, LS, D])
    sC = k.din("sC", [NSS, 4, 128, 128])
    sn = k.din("sn", [NSS, 4, 128])
    sm = k.din("sm", [NSS, 4])
    sS = k.din("sS", [NSS, 4, 128, 128])
    scv = k.din("scv", [NSS, 3, 1536])
    ck = k.din("ck", [NSS, NMEM, D])
    cv = k.din("cv", [NSS, NMEM, D])
    memp = k.din("memp", [NPS, NMEM, D])
    w_in = k.din("w_in", [D, 4112])
    w_out = k.din("w_out", [D, D])
    wq = k.din("wq", [D, D])
    wk = k.din("wk", [D, D])
    wv = k.din("wv", [D, D])
    wo = k.din("wo", [D, D])
    w1 = k.din("w1", [D, 4096])
    w2 = k.din("w2", [4096, D])
    nw_mix = k.din("nw_mix", [D])
    nw_x = k.din("nw_x", [D])
    nw_mem = k.din("nw_mem", [D])
    nw_ffn = k.din("nw_ffn", [D])
    nw_fin = k.din("nw_fin", [D])
    b_ig = k.din("b_ig", [4])
    b_fg = k.din("b_fg", [4])
    nw_a = k.din("nw_a", [128])
    nw_b = k.din("nw_b", [128])
    convw = k.din("convw", [4, 1536])
    a_log = k.din("a_log", [4])
    dt_b = k.din("dt_b", [4])

    yp = k.dout("yp", [NPS, SEQ, D])
    ys = k.dout("ys", [NSS, LS, D])
    o_pC = k.dout("o_pC", [NPS, 4, 128, 128])
    o_pn = k.dout("o_pn", [NPS, 4, 128])
    o_pm = k.dout("o_pm", [NPS, 4])
    o_pS = k.dout("o_pS", [NPS, 4, 128, 128])
    o_pcv = k.dout("o_pcv", [NPS, 3, 1536])
    o_pmk = k.dout("o_pmk", [NPS, NMEM, D])
    o_pmv = k.dout("o_pmv", [NPS, NMEM, D])
    o_sC = k.dout("o_sC", [NSS, 4, 128, 128])
    o_sn = k.dout("o_sn", [NSS, 4, 128])
    o_sm = k.dout("o_sm", [NSS, 4])
    o_sS = k.dout("o_sS", [NSS, 4, 128, 128])
    o_scv = k.dout("o_scv", [NSS, 3, 1536])

    scr = {}
    for nm, nb in (("in", 8), ("out", 2), ("q", 2), ("k", 2), ("v", 2), ("o", 2), ("1", 8), ("2", 8)):
        scr[nm] = k.dint("scr_" + nm, [nb, 128, 4096], BF16)

    k.setup_engines()
    PE, ACT, DVE, POOL, SP = k.PE, k.ACT, k.DVE, k.POOL, k.SP

    X = k.sb("X", [128, 4, D])
    XN = k.sb("XN", [128, 2, D], BF16)
    XNT = k.sb("XNT", [128, 8, 512], BF16)
    WR = k.sb("WR", [128, NSLOT, 8, 512], BF16)
    GW = k.sb("GW", [128, 8, 16], BF16)
    ARA = k.sb("ARA", [128, 8192])
    ARM = k.sb("ARM", [128, 8192])
    QKA = ARA[:, 0:4096].rearrange("p (c n) -> p c n", c=8)
    KA = ARA[:, 4096:6144].rearrange("p (c n) -> p c n", c=4)
    OG = ARA[:, 6144:7168].bitcast(BF16).rearrange("p (c n) -> p c n", c=4)
    ZG = ARA[:, 7168:8192].bitcast(BF16).rearrange("p (c n) -> p c n", c=4)
    POST = ARM[:, 0:6144].rearrange("p (c n) -> p c n", c=12)
    POSTF = POST
    PRE = ARM[:, 6144:6144 + 3 * 515].rearrange("p (c n) -> p c n", c=3)
    QF = ARA[:, 0:2048].bitcast(BF16).rearrange("p (c n) -> p c n", c=8)
    PTS = ARA[:, 2048:4096].bitcast(BF16).rearrange("p (c n) -> p c n", c=8)
    OTB = ARA[:, 4096:6144].bitcast(BF16).rearrange("p (c n) -> p c n", c=8)
    PF = ARA[:, 6144:7168]
    PN = ARA[:, 7168:7680].bitcast(BF16)
    HID = ARM[:, :].bitcast(BF16).rearrange("p (c n) -> p c n", c=32)
    VA = k.sb("VA", [128, 4, 4, 130], F32R)
    GT = k.sb("GT", [128, 4, 16])
    HT = k.sb("HT", [128, 8, 512], BF16)
    H = k.sb("H", [128, 1, D], BF16)
    HIST4 = k.sb("HIST4", [128, 4, 12, 3])
    DG = k.sb("DG", [4, 4])
    TS = k.sb("TS", [48, 128])
    HS = k.sb("HS", [128, 36])
    dHS = Dep()
    dTS = Dep()
    BLG = k.sb("BLG", [128, 8])
    BLGb = k.sb("BLGb", [128, 8])
    BLG2 = [BLG, BLGb]
    KT = k.sb("KT", [128, 8, NMEM], BF16)
    VV = k.sb("VV", [128, 2, D], BF16)
    CA = k.sb("CA", [128, 2, 4, 130], F32R)
    SS = k.sb("SS", [128, 2, 4, 128], F32R)
    ident = k.sb("ident", [128, 128])
    identb = k.sb("identb", [128, 128], BF16)
    tri = k.sb("tri", [128, 128])
    maskA = k.sb("maskA", [128, 128])
    mstrict = k.sb("mstrict", [128, 128])
    ones = k.sb("ones", [128, 128])
    WBF = k.sb("WBF", [128, D])
    WN = k.sb("WN", [128, 4, 8])
    NWA = k.sb("NWA", [128, 128])
    NWB = k.sb("NWB", [128, 128])
    GB = k.sb("GB", [128, 16])
    SGN = k.sb("SGN", [128, 16])
    NEG8 = k.sb("NEG8", [128, 8])
    WCV = k.sb("WCV", [128, 12, 4])
    ST = k.sb("ST", [128, 64])
    SM = k.sb("SM", [128, 256])
    JUNK = k.sb("JUNK", [128, D], BF16)
    TT = [k.sb("T%d" % i, [128, 512]) for i in range(5)]
    dTT = [Dep() for _ in range(5)]
    QSR, SMKR, QGR, QKMR = [k.sb(n_, [128, 512], F32R) for n_ in ("QSR", "SMKR", "QGR", "QKMR")]
    dQSR, dSMKR, dQGR, dQKMR = Dep(), Dep(), Dep(), Dep()
    RB = k.sb("RB", [128, 8, 128])
    NY = k.sb("NY", [128, 1, 4, 128 if NF32 < 7 else 2], BF16)
    NZ = k.sb("NZ", [128, 2, 4, 128 if NF32 < 7 else 2], BF16)
    NP_ = k.sb("NP", [128, 1, 4, 128], F32R)
    NPB = k.sb("NPB", [128, 1, 4, 128 if NF32 < 7 else 2], BF16)
    dNPB = Dep()
    NYF = k.sb("NYF", [128, 4, 128], F32R)
    NZF = k.sb("NZF", [128, 2, 4, 128], F32R)
    dNYF = Dep()
    dNZF = [Dep(), Dep()]
    dNYFh = [Dep(), Dep()]
    dNZFh = [[Dep(), Dep()], [Dep(), Dep()]]
    dNPh = [Dep(), Dep()]
    GK = k.sb("GK", [128, 4, 128], F32R)
    KD = k.sb("KD", [128, 4, 128], F32R)
    VB = k.sb("VB", [128, 4, 128], F32R)
    BU = k.sb("BU", [128, 4, 128], F32R)
    WKT = k.sb("WKT", [128, 4, 128], F32R)
    KW = k.sb("KW", [128, 4, 128], F32R)
    M4 = k.sb("M4", [4, 8])
    BBC = k.sb("BBC", [128, 4])
    M0B = k.sb("M0B", [128, 4])
    STG = k.sb("STG", [128, 4, 129])
    AG = k.sb("AG", [128, 8])

    PT = [k.psum("PT%d" % i, [128, 1024], BF16) for i in range(2)]
    PS = [k.psum("PS%d" % i, [128, 512]) for i in range(6)]
    ptd = [Dep(True), Dep(True)]
    psd = [Dep(True) for _ in range(6)]
    rr = {"pt": 0, "ps": 0, "tt": 0, "pre": 0}

    def pt():
        i = rr["pt"] % 2
        rr["pt"] += 1
        return PT[i], ptd[i]

    def ps():
        i = rr["ps"] % 6
        rr["ps"] += 1
        return PS[i], psd[i]

    def psA():
        i = rr.setdefault("psa", 0) % 3
        rr["psa"] += 1
        return PS[i], psd[i]

    def psB():
        i = 3 + rr.setdefault("psb", 0) % 3
        rr["psb"] += 1
        return PS[i], psd[i]

    def tmp():
        i = rr["tt"] % 5
        rr["tt"] += 1
        return TT[i], dTT[i]

    dX = [Dep() for _ in range(4)]
    dST = [Dep(), Dep()]
    dSTF = [Dep(), Dep()]
    dXN = [Dep(), Dep()]
    dXNT = Dep()
    dWR = [Dep() for _ in range(NSLOT)]
    dconst = Dep()

    def pool(name, reads=(), writes=(), **kw):
        return k.vec(POOL, name, reads, writes, **kw)

    def dve(name, reads=(), writes=(), **kw):
        return k.vec(DVE, name, reads, writes, **kw)

    pool("memset", writes=[dconst], ap=ones[:], constant=1.0)
    pool("memset", writes=[dconst], ap=ident[:], constant=1.0)
    k.op(POOL, lambda: nc.gpsimd.affine_select(out=ident[:], in_=ident[:], pattern=[[-1, 128]], compare_op=ALU.is_equal,
                                                fill=0.0, base=0, channel_multiplier=1), [dconst], [dconst])
    pool("tensor_copy", [dconst], [dconst], out=identb[:], in_=ident[:])
    pool("memset", writes=[dconst], ap=tri[:], constant=1.0)
    k.op(POOL, lambda: nc.gpsimd.affine_select(out=tri[:], in_=tri[:], pattern=[[1, 128]], compare_op=ALU.is_ge,
                                                fill=0.0, base=0, channel_multiplier=-1), [dconst], [dconst])
    pool("memset", writes=[dconst], ap=mstrict[:], constant=1.0)
    k.op(POOL, lambda: nc.gpsimd.affine_select(out=mstrict[:], in_=mstrict[:], pattern=[[1, 128]], compare_op=ALU.is_gt,
                                                fill=0.0, base=0, channel_multiplier=-1), [dconst], [dconst])
    pool("tensor_scalar_mul", [dconst], [dconst], out=maskA[:], in0=tri[:], scalar1=128.0 ** -0.5)
    pool("memset", writes=[dconst], ap=SGN[:], constant=1.0)
    pool("memset", writes=[dconst], ap=SGN[:, 4:8], constant=-1.0)
    pool("memset", writes=[dconst], ap=SGN[:, 12:16], constant=-1.0)
    pool("memset", writes=[dconst], ap=GB[:], constant=0.0)
    pool("memset", writes=[dconst], ap=NEG8[:], constant=-1.0)

    k.dma(SP, WBF[:], nw_fin.partition_broadcast(128), writes=[dconst])
    def load_T(dst_view, src2d, rows, wdeps=()):
        k.dma(SP, TS[:rows, :], src2d, writes=[dTS])
        p, dp = ps()
        k.tr(p[:, :rows], TS[:rows, :], ident[:rows, :rows], [dTS, dconst], [dp])
        dve("tensor_copy", [dp], [dconst] + list(wdeps), out=dst_view, in_=p[:, :rows])

    def store_T(dst2d, src_view, rows, rdeps):
        p, dp = ps()
        k.tr(p[:rows, :128], src_view, ident[:, :], list(rdeps) + [dconst], [dp])
        k.act(TS[:rows, :], p[:rows, :128], AF.Copy, [dp], [dTS])
        k.dma(SP, dst2d, TS[:rows, :], reads=[dTS])

    k.dma(SP, NWA[:], nw_a.partition_broadcast(128), writes=[dconst])
    k.dma(SP, NWB[:], nw_b.partition_broadcast(128), writes=[dconst])
    k.dma(SP, GB[:, 0:4], b_ig.partition_broadcast(128), writes=[dconst])
    k.dma(SP, GB[:, 4:8], b_fg.partition_broadcast(128), writes=[dconst])
    k.dma(SP, GB[:, 8:12], dt_b.partition_broadcast(128), writes=[dconst])
    k.dma(SP, ST[:, 0:4], a_log.partition_broadcast(128), writes=[dconst])
    for i, src in enumerate((nw_mix, nw_x, nw_ffn, nw_mem)):
        load_T(WN[:, i, :], src.rearrange("(k p) -> k p", p=128), 8)
    load_T(WCV[:, :, :].rearrange("p c j -> p j c"), convw.rearrange("j (c p) -> (j c) p", p=128), 48)
    k.act(ST[:, 4:8], ST[:, 0:4], AF.Exp, [dconst], [dconst])
    dve("tensor_scalar_mul", [dconst], [dconst], out=NEG8[:, 4:8], in0=ST[:, 4:8], scalar1=-1.0)
    ones16 = ones[:, 0:16].rearrange("p (a b c) -> p a b c", a=4, b=4)
    dve("tensor_copy", [dconst], [dconst], out=VA[:, :, :, 128:129], in_=ones16)
    dve("tensor_scalar_mul", [dconst], [dconst], out=VA[:, :, :, 129:130], in0=ones16, scalar1=0.0)

    dscr = {}

    def cast_blk(nm, b_, src_rows_cols):
        d_ = dscr.setdefault((nm, b_), Dep())
        k.dma(POOL, scr[nm][b_].rearrange("p (k c) -> p k c", k=8), src_rows_cols.rearrange("(k p) c -> p k c", p=128),
              writes=[d_])

    in_cols = [0, 512, 1024, 1536, 2056, 2568, 3080, 3592]
    if stage < 2:
        for b in range(2):
            cast_blk("k", b, wk[:, b * 512:(b + 1) * 512])
        for b in range(2):
            cast_blk("v", b, wv[:, b * 512:(b + 1) * 512])
    if stage >= 2:
        for b in (4, 0, 5, 1, 6, 2, 3, 7):
            cast_blk("in", b, w_in[:, in_cols[b]:in_cols[b] + 512])
        for kc in range(8):
            k.dma(POOL, GW[:, kc, 0:8], w_in[kc * 128:(kc + 1) * 128, 2048:2056], writes=[dconst])
            k.dma(POOL, GW[:, kc, 8:16], w_in[kc * 128:(kc + 1) * 128, 4104:4112], writes=[dconst])
        for nm, src in (("out", w_out), ("q", wq), ("o", wo)):
            for b in range(2):
                cast_blk(nm, b, src[:, b * 512:(b + 1) * 512])
        for b in range(8):
            cast_blk("1", b, w1[:, b * 512:(b + 1) * 512])
        for cb in range(2):
            for fb in range(4):
                cast_blk("2", cb * 4 + fb, w2[fb * 1024:(fb + 1) * 1024, cb * 512:(cb + 1) * 512])
        for b in range(2):
            cast_blk("k", b, wk[:, b * 512:(b + 1) * 512])
        for b in range(2):
            cast_blk("v", b, wv[:, b * 512:(b + 1) * 512])

    grp_seq = [("in", b) for b in (4, 0, 5, 1, 6, 2, 3, 7)] + [("out", 0), ("out", 1), ("q", 0), ("q", 1), ("o", 0), ("o", 1)] \
        + [("1", b) for b in range(8)] + [("2", b) for b in range(8)]
    mem_seq = [("k", 0), ("k", 1), ("v", 0), ("v", 1)]
    NGP = SEQ // 512
    KNG = int(os.environ.get("KNG", str(NGP)))
    if stage >= 2:
        wseq = list(grp_seq)
        for j in range(NPS):
            wseq += mem_seq
            for g in range(KNG):
                wseq += grp_seq
    else:
        wseq = mem_seq + mem_seq
    wst = {"issued": 0, "used": 0}

    def wblk(nm, b, ahead=NSLOT - 1):
        i = wst["used"]
        assert wseq[i] == (nm, b), (i, wseq[i], nm, b)
        while wst["issued"] < min(i + 1 + ahead, len(wseq)):
            ii = wst["issued"]
            nm2, b2 = wseq[ii]
            slot2 = ii % NSLOT
            k.dma(SP, WR[:, slot2].rearrange("p k c -> p (k c)"), scr[nm2][b2], reads=[dscr[(nm2, b2)]], writes=[dWR[slot2]])
            wst["issued"] += 1
        wst["used"] += 1
        slot = i % NSLOT
        return WR[:, slot], dWR[slot]

    def norm_pre(src_ap, dsrc, L, par, extra_reads=()):
        dst_ = dST[par]
        k.act(JUNK[:L, :], src_ap, AF.Square, [dsrc] + list(extra_reads), [dst_], accum=ST[:L, 8 + par:9 + par])
        k.act(ST[:L, 10 + par:11 + par], ST[:L, 8 + par:9 + par], AF.Ln, [dst_], [dst_], bias=EPS, scale=1.0 / D)
        k.act(ST[:L, 12 + par:13 + par], ST[:L, 10 + par:11 + par], AF.Exp, [dst_], [dst_], scale=-0.5)
        dve("tensor_scalar_mul", [dsrc, dst_] + list(extra_reads), [dXN[par]], out=XN[:L, par, :], in0=src_ap,
            scalar1=ST[:L, 12 + par:13 + par])

    def norm_tr(L, col, wn_idx, par, dst=None, ddst=None):
        if dst is None:
            dst, ddst = XNT, dXNT
        p, dp = pt()
        pv = p[:].rearrange("p (k c) -> p k c", k=8)
        for kc in range(8):
            k.tr(pv[:, kc, :L], XN[:L, par, kc * 128:(kc + 1) * 128], identb[:L, :L], [dXN[par], dconst], [dp],
                 inc=(kc == 7))
        dve("tensor_tensor", [dp, dconst], [ddst], out=dst[:, :, col:col + L], in0=pv[:, :, :L],
            in1=WN[:, wn_idx, :].unsqueeze(2).to_broadcast([128, 8, L]), op=ALU.mult)

    def rmsnorm_to_xnt(src_ap, dsrc, L, col, wn_idx, par, dst=None, ddst=None, extra_reads=()):
        norm_pre(src_ap, dsrc, L, par, extra_reads)
        norm_tr(L, col, wn_idx, par, dst, ddst)

    def load_x(i, src_ap, L):
        k.dma(SP, X[:L, i, :], src_ap, writes=[dX[i]])

    dKT = Dep()
    dVV = Dep()
    MT = HT
    dMT = Dep()

    def mem_kv(j):
        k.barrier()
        for t in range(2):
            load_x(t, memp[j, t * 128:(t + 1) * 128, :], 128)
            rmsnorm_to_xnt(X[:, t, :], dX[t], 128, t * 128, 3, t, dst=MT, ddst=dMT)
        for which, out_d in (("k", o_pmk), ("v", o_pmv)):
            for cb in range(2):
                w, dw = wblk(which, cb)
                for t in range(2):
                    p, dp = ps()
                    for kc in range(8):
                        k.mm(p[:, :], MT[:, kc, t * 128:(t + 1) * 128], w[:, kc, :], kc == 0, kc == 7, [dMT, dw], [dp])
                    tt, dtt = tmp()
                    k.act(tt[:, :], p[:, :], AF.Copy, [dp], [dtt])
                    k.dma(SP, out_d[j, t * 128:(t + 1) * 128, cb * 512:(cb + 1) * 512], tt[:, :], reads=[dtt])
                    if which == "v":
                        dve("tensor_copy", [dp], [dVV], out=VV[:, t, cb * 512:(cb + 1) * 512], in_=p[:, :])
                if which == "k":
                    for jj in range(4):
                        p, dp = ps()
                        for kc in range(8):
                            k.mm(p[:, :NMEM], w[:, kc, jj * 128:(jj + 1) * 128], MT[:, kc, :NMEM], kc == 0, kc == 7,
                                 [dMT, dw], [dp])
                        k.act(KT[:, cb * 4 + jj, :], p[:, :NMEM], AF.Copy, [dp], [dKT])
        k.barrier()

    def sample_kv(j):
        for t in range(2):
            for half in range(2):
                tk_, dtk = tmp()
                k.dma(SP, tk_[:, :], ck[j, t * 128:(t + 1) * 128, half * 512:(half + 1) * 512], writes=[dtk])
                dve("tensor_copy", [dtk], [dXN[0]], out=XN[:, 0, half * 512:(half + 1) * 512], in_=tk_[:, :])
                tv_, dtv = tmp()
                k.dma(SP, tv_[:, :], cv[j, t * 128:(t + 1) * 128, half * 512:(half + 1) * 512], writes=[dtv])
                dve("tensor_copy", [dtv], [dVV], out=VV[:, t, half * 512:(half + 1) * 512], in_=tv_[:, :])
            p, dp = pt()
            pv = p[:].rearrange("p (k c) -> p k c", k=8)
            for kc in range(8):
                k.tr(pv[:, kc, :], XN[:, 0, kc * 128:(kc + 1) * 128], identb[:, :], [dXN[0], dconst], [dp],
                     inc=(kc == 7))
            k.act(KT[:, :, t * 128:(t + 1) * 128], pv[:, :, :], AF.Copy, [dp], [dKT])

    dQKA = [Dep() for _ in range(8)]
    dKA = [Dep() for _ in range(4)]
    dVA = [Dep() for _ in range(4)]
    dOG = [Dep() for _ in range(4)]
    dZG = [Dep() for _ in range(4)]
    dGT = [Dep() for _ in range(4)]
    dPOST = [Dep() for _ in range(12)]
    dPRE = [Dep() for _ in range(3)]
    dHIST = [[Dep() for _ in range(12)] for _ in range(4)]
    dH = [Dep(), Dep()]
    dHT = Dep()
    dCA = [Dep(), Dep()]
    dSS = [Dep(), Dep()]
    dSMALL = Dep()
    dSg, dSa, dSb = Dep(), Dep(), Dep()
    dSgp = [Dep(), Dep()]
    dBLGp = [Dep(), Dep()]
    dRBg = Dep()
    dM4 = Dep()
    dBBC = Dep()
    state = {"cur": 0}

    class Tile:
        def __init__(self, L, col, src, dst, seg):
            self.L, self.col, self.src, self.dst, self.seg = L, col, src, dst, seg

    class Seg:
        def __init__(self, kind, j, col0, n, first, last, sidx=0):
            self.kind, self.j, self.col0, self.n, self.first, self.last, self.sidx = kind, j, col0, n, first, last, sidx

    def proj_tm(w, dw, t, ncols=512):
        p, dp = ps()
        for kc in range(8):
            k.mm(p[:t.L, :ncols], XNT[:, kc, t.col:t.col + t.L], w[:, kc, :ncols], kc == 0, kc == 7, [dXNT, dw], [dp])
        return p, dp

    def proj_fm(w, dw, j, NT, src=None, dsrc=None):
        if src is None:
            src, dsrc = XNT, dXNT
        p, dp = ps()
        for kc in range(8):
            k.mm(p[:, :NT], w[:, kc, j * 128:(j + 1) * 128], src[:, kc, :NT], kc == 0, kc == 7, [dsrc, dw], [dp])
        return p, dp

    def interleave(gens, K):
        active = []
        it = iter(gens)
        while True:
            while len(active) < K:
                g = next(it, None)
                if g is None:
                    break
                active.append(g)
            if not active:
                break
            for g in list(active):
                try:
                    next(g)
                except StopIteration:
                    active.remove(g)

    def sigmoid_gen(w, dw, t, out_view, dout, nw, mul_in=False):
        L = t.L
        p, dp = proj_tm(w, dw, t)
        tt, dtt = tmp()
        k.act(tt[:L, :], p[:L, :], AF.Exp, [dp], [dtt], scale=-1.0)
        yield
        k.act(tt[:L, :], tt[:L, :], AF.Ln, [dtt], [dtt], bias=1.0)
        yield
        k.act(tt[:L, :], tt[:L, :], AF.Exp, [dtt], [dtt], scale=-1.0)
        yield
        if mul_in:
            dve("tensor_tensor", [dtt, dp], [dtt], out=tt[:L, :], in0=p[:L, :], in1=tt[:L, :], op=ALU.mult)
        yield
        pool("tensor_tensor", [dtt, dconst], [dout], out=out_view,
             in0=tt[:L, :].rearrange("p (h d) -> p h d", h=4),
             in1=nw[:L, :].unsqueeze(1).to_broadcast([L, 4, 128]), op=ALU.mult)

    def conv_chunk(c, p, dp, seg):
        for _ in conv_gen(c, p, dp, seg):
            pass

    def conv_gen(c, p, dp, seg):
        n, c0 = seg.n, seg.col0
        HIST = HIST4[:, seg.sidx]
        dH_ = dHIST[seg.sidx]
        r = rr["pre"] % 3
        rr["pre"] += 1
        pre = PRE[:, r, :]
        k.act(pre[:, 3:3 + n], p[:, c0:c0 + n], AF.Copy, [dp], [dPRE[r]])
        pool("tensor_copy", [dH_[c]], [dPRE[r]], out=pre[:, 0:3], in_=HIST[:, c, :])
        pool("tensor_copy", [dPRE[r]], [dH_[c]], out=HIST[:, c, :], in_=pre[:, n:n + 3])
        po = POST[:, c, c0:c0 + n]
        yield
        dve("tensor_scalar_mul", [dPRE[r], dconst], [dPOST[c]], out=po, in0=pre[:, 0:n], scalar1=WCV[:, c, 0:1])
        for j in range(1, 4):
            dve("scalar_tensor_tensor", [dPRE[r], dconst, dPOST[c]], [dPOST[c]], out=po, in0=pre[:, j:j + n],
                 scalar=WCV[:, c, j:j + 1], in1=po, op0=ALU.mult, op1=ALU.add)
        yield
        tt, dtt = tmp()
        k.act(tt[:, :n], po, AF.Exp, [dPOST[c]], [dtt], scale=-1.0)
        yield
        k.act(tt[:, :n], tt[:, :n], AF.Ln, [dtt], [dtt], bias=1.0)
        yield
        k.act(tt[:, :n], tt[:, :n], AF.Exp, [dtt], [dtt], scale=-1.0)
        yield
        pool("tensor_tensor", [dtt, dPOST[c]], [dPOST[c]], out=po, in0=po, in1=tt[:, :n], op=ALU.mult)
        if c < 8:
            yield
            t2, dt2 = tmp()
            k.act(t2[:, :n], po, AF.Square, [dPOST[c]], [dt2])
            p2, dp2 = ps()
            sc = 128.0 if c < 4 else 1.0
            k.mm(p2[:, :n], ones[:, :], t2[:, :n], True, True, [dt2, dconst], [dp2])
            yield
            k.act(t2[:, :n], p2[:, :n], AF.Ln, [dp2], [dt2], bias=EPS * sc, scale=sc)
            k.act(t2[:, :n], t2[:, :n], AF.Exp, [dt2], [dt2], scale=-0.5)
            yield
            pool("tensor_tensor", [dt2, dPOST[c]], [dPOST[c]], out=po, in0=po, in1=t2[:, :n], op=ALU.mult)

    free_ps = list(range(6))
    free_tt = list(range(5))
    free_pre = list(range(3))

    XP = ARA[:, 4096:8192].rearrange("p (c n) -> p c n", c=4)
    dXP = [Dep() for _ in range(4)]

    def prefetch_x(next_tiles):
        for E in k.engs:
            if E.cnt > 0:
                k.wait(SP, (E.sem, E.cnt))
        for i, t in enumerate(next_tiles):
            k.dma(SP, XP[:t.L, i, :], t.src, writes=[dXP[i]])

    def phase_in(tiles, segs, NT, pref=False):
        for i, t in enumerate(tiles):
            if pref:
                if i < 2:
                    pool("tensor_copy", [dXP[i]] + dKA + dOG + dZG, [dX[i]], out=X[:t.L, i, :], in_=XP[:t.L, i, :])
                else:
                    k.act(X[:t.L, i, :], XP[:t.L, i, :], AF.Copy, [dXP[i]] + dKA + dOG + dZG, [dX[i]])
            else:
                load_x(i, t.src, t.L)
                rmsnorm_to_xnt(X[:t.L, i, :], dX[i], t.L, t.col, 0, i % 2)
        single = (len(segs) == 1)
        wcache = {}

        def getw(blk):
            if blk not in wcache:
                wcache[blk] = wblk("in", blk, ahead=1)
            return wcache[blk]

        def acq(lst):
            while not lst:
                yield
            return lst.pop(0)

        def fm_gen(blk, h, dst_chunk):
            w, dw = getw(blk)
            bi = yield from acq(free_ps)
            p, dp = PS[bi], psd[bi]
            for kc in range(8):
                k.mm(p[:, :NT], w[:, kc, h * 128:(h + 1) * 128], XNT[:, kc, :NT], kc == 0, kc == 7, [dXNT, dw], [dp])
            k.act(QKA[:, dst_chunk, :NT], p[:, :NT], AF.Copy, [dp], [dQKA[dst_chunk]])
            free_ps.append(bi)
            yield

        def tm_gen(blk, i, t, kind):
            L = t.L
            bi = yield from acq(free_ps)
            p, dp = PS[bi], psd[bi]
            if kind == "gates":
                for kc in range(8):
                    k.mm(p[:L, :16], XNT[:, kc, t.col:t.col + L], GW[:, kc, :], kc == 0, kc == 7, [dXNT, dconst], [dp])
                k.act(GT[:L, i, :], p[:L, :16], AF.Copy, [dp], [dGT[i]])
                free_ps.append(bi)
                yield
                return
            w, dw = getw(blk)
            for kc in range(8):
                k.mm(p[:L, :], XNT[:, kc, t.col:t.col + L], w[:, kc, :], kc == 0, kc == 7, [dXNT, dw], [dp])
            if kind == "ka":
                k.act(KA[:L, i, :], p[:L, :], AF.Copy, [dp], [dKA[i]])
                free_ps.append(bi)
                yield
                return
            if kind == "va":
                k.act(VA[:L, i, :, 0:128], p[:L, :].rearrange("p (h d) -> p h d", h=4), AF.Copy, [dp], [dVA[i]])
                free_ps.append(bi)
                yield
                return
            ti = yield from acq(free_tt)
            tt, dtt = TT[ti], dTT[ti]
            k.act(tt[:L, :], p[:L, :], AF.Exp, [dp], [dtt], scale=-1.0)
            if kind == "oa":
                free_ps.append(bi)
            yield
            k.act(tt[:L, :], tt[:L, :], AF.Ln, [dtt], [dtt], bias=1.0)
            yield
            k.act(tt[:L, :], tt[:L, :], AF.Exp, [dtt], [dtt], scale=-1.0)
            yield
            if kind == "zb":
                dve("tensor_tensor", [dtt, dp], [dtt], out=tt[:L, :], in0=p[:L, :], in1=tt[:L, :], op=ALU.mult)
                free_ps.append(bi)
                yield
            outv, dout, nw = (OG, dOG, NWA) if kind == "oa" else (ZG, dZG, NWB)
            pool("tensor_tensor", [dtt, dconst], [dout[i]], out=outv[:L, i, :].rearrange("p (h d) -> p h d", h=4),
                 in0=tt[:L, :].rearrange("p (h d) -> p h d", h=4),
                 in1=nw[:L, :].unsqueeze(1).to_broadcast([L, 4, 128]), op=ALU.mult)
            free_tt.append(ti)
            yield

        def conv_item(c):
            w, dw = getw(4 + c // 4)
            bi = yield from acq(free_ps)
            p, dp = PS[bi], psd[bi]
            h = c % 4
            for kc in range(8):
                k.mm(p[:, :NT], w[:, kc, h * 128:(h + 1) * 128], XNT[:, kc, :NT], kc == 0, kc == 7, [dXNT, dw], [dp])
            for seg in segs:
                n, c0 = seg.n, seg.col0
                HIST = HIST4[:, seg.sidx]
                dH_ = dHIST[seg.sidx]
                r = yield from acq(free_pre)
                pre = PRE[:, r, :]
                k.act(pre[:, 3:3 + n], p[:, c0:c0 + n], AF.Copy, [dp], [dPRE[r]])
                if seg is segs[-1]:
                    free_ps.append(bi)
                pool("tensor_copy", [dH_[c]], [dPRE[r]], out=pre[:, 0:3], in_=HIST[:, c, :])
                pool("tensor_copy", [dPRE[r]], [dH_[c]], out=HIST[:, c, :], in_=pre[:, n:n + 3])
                po = POST[:, c, c0:c0 + n]
                yield
                dve("tensor_scalar_mul", [dPRE[r], dconst], [dPOST[c]], out=po, in0=pre[:, 0:n], scalar1=WCV[:, c, 0:1])
                for j in range(1, 4):
                    dve("scalar_tensor_tensor", [dPRE[r], dconst, dPOST[c]], [dPOST[c]], out=po, in0=pre[:, j:j + n],
                        scalar=WCV[:, c, j:j + 1], in1=po, op0=ALU.mult, op1=ALU.add)
                free_pre.append(r)
                yield
                ti = yield from acq(free_tt)
                tt, dtt = TT[ti], dTT[ti]
                k.act(tt[:, :n], po, AF.Exp, [dPOST[c]], [dtt], scale=-1.0)
                yield
                k.act(tt[:, :n], tt[:, :n], AF.Ln, [dtt], [dtt], bias=1.0)
                yield
                k.act(tt[:, :n], tt[:, :n], AF.Exp, [dtt], [dtt], scale=-1.0)
                yield
                pool("tensor_tensor", [dtt, dPOST[c]], [dPOST[c]], out=po, in0=po, in1=tt[:, :n], op=ALU.mult)
                if c < 8:
                    yield
                    k.act(tt[:, :n], po, AF.Square, [dPOST[c]], [dtt])
                    b2 = yield from acq(free_ps)
                    p2, dp2 = PS[b2], psd[b2]
                    sc = 128.0 if c < 4 else 1.0
                    k.mm(p2[:, :n], ones[:, :], tt[:, :n], True, True, [dtt, dconst], [dp2])
                    yield
                    k.act(tt[:, :n], p2[:, :n], AF.Ln, [dp2], [dtt], bias=EPS * sc, scale=sc)
                    free_ps.append(b2)
                    yield
                    k.act(tt[:, :n], tt[:, :n], AF.Exp, [dtt], [dtt], scale=-0.5)
                    yield
                    pool("tensor_tensor", [dtt, dPOST[c]], [dPOST[c]], out=po, in0=po, in1=tt[:, :n], op=ALU.mult)
                free_tt.append(ti)
                yield

        def alt(a, b_):
            out = []
            for i_ in range(max(len(a), len(b_))):
                if i_ < len(a):
                    out.append(a[i_])
                if i_ < len(b_):
                    out.append(b_[i_])
            return out

        T_ = list(enumerate(tiles))
        items = []
        items += alt([conv_item(c) for c in range(0, 4)], [fm_gen(0, h, h) for h in range(4)])
        items += alt([conv_item(c) for c in range(4, 8)],
                     [fm_gen(1, h, 4 + h) for h in range(4)] + [tm_gen(1, i, t, "ka") for i, t in T_])
        items += alt([conv_item(c) for c in range(8, 12)], [tm_gen(2, i, t, "va") for i, t in T_])
        items += alt([tm_gen(3, i, t, "oa") for i, t in T_], [tm_gen(None, i, t, "gates") for i, t in T_])
        items += [tm_gen(7, i, t, "zb") for i, t in T_]
        interleave(items, 6 if single else 2)
        assert len(free_ps) == 6 and len(free_tt) == 5 and len(free_pre) == 3

    def smc(a, b, L):
        return SM[:L, a:b]

    def hist_init(seg):
        HIST = HIST4[:, seg.sidx]
        dH_ = dHIST[seg.sidx]
        if seg.kind == "p":
            for c in range(12):
                pool("memset", writes=[dH_[c]], ap=HIST[:, c, :], constant=0.0)
        else:
            load_T(HIST[:, :, :].rearrange("p c t -> p t c"), scv[seg.j].rearrange("t (c p) -> (t c) p", p=128), 36, dH_)

    def state_init(seg):
        cur = state["cur"]
        if seg.kind == "p":
            dve("tensor_scalar_mul", [dconst], [dCA[cur]], out=CA[:, cur], scalar1=0.0,
                in0=ones[:, 0:4].unsqueeze(2).to_broadcast([128, 4, 130]))
            dve("tensor_scalar_mul", [dconst], [dSS[cur]], out=SS[:, cur], scalar1=0.0,
                in0=ones[:, 0:4].unsqueeze(2).to_broadcast([128, 4, 128]))
            pool("memset", writes=[dM4], ap=M4[:, :], constant=0.0)
            pool("memset", writes=[dBBC], ap=BBC[:, :], constant=0.0)
        else:
            j = seg.j
            k.dma(SP, STG[:, :, 0:128], sC[j].rearrange("h a b -> a h b"), writes=[dSTG])
            load_T(STG[:, :, 128], sn[j], 4, [dSTG])
            tss, dtss = tmp()
            k.dma(SP, tss[:, :].rearrange("p (h d) -> p h d", h=4), sS[j].rearrange("h a b -> a h b"), writes=[dtss])
            dve("tensor_copy", [dtss], [dSS[cur]], out=SS[:, cur], in_=tss[:, :].rearrange("p (h d) -> p h d", h=4))
            k.dma(SP, M0B[:, :], sm[j].partition_broadcast(128), writes=[dM0B])
            pool("memset", writes=[dM4], ap=M4[:, :], constant=0.0)
            k.dma(SP, M4[0:4, 0:1], sm[j].rearrange("(h o) -> h o", o=1), writes=[dM4])
            pool("memset", writes=[dBBC], ap=BBC[:, :], constant=0.0)
            k.act(M0B[:, :], M0B[:, :], AF.Exp, [dM0B], [dM0B])
            dve("tensor_tensor", [dSTG, dM0B], [dCA[cur]], out=CA[:, cur, :, 0:129], in0=STG[:, :, :],
                in1=M0B[:, :].unsqueeze(2).to_broadcast([128, 4, 129]), op=ALU.mult)
            dve("tensor_scalar_mul", [dconst], [dCA[cur]], out=CA[:, cur, :, 129:130], scalar1=0.0,
                in0=ones[:, 0:4].unsqueeze(2))

    def seq_final(seg):
        cur = state["cur"]
        j = seg.j
        oC, on, om, oS, ocv = (o_pC, o_pn, o_pm, o_pS, o_pcv) if seg.kind == "p" else (o_sC, o_sn, o_sm, o_sS, o_scv)
        dve("tensor_tensor", [dM4], [dM4], out=M4[:, 4:5], in0=M4[:, 0:1], in1=M4[:, 1:2], op=ALU.add)
        k.dma(SP, om[j].rearrange("(h o) -> h o", o=1), M4[:, 4:5], reads=[dM4])
        dve("tensor_scalar_mul", [dM4, dconst], [dDG], out=DG[:, :], in0=ident[0:4, 0:4], scalar1=M4[:, 4:5])
        p, dp = ps()
        k.mm(p[:, 0:4], ones[0:4, :], DG[:, :], True, True, [dDG, dconst], [dp])
        k.act(M0B[:, :], p[:, 0:4], AF.Exp, [dp], [dM0B], scale=-1.0)
        dve("tensor_tensor", [dCA[cur], dM0B], [dSTG], out=STG[:, :, :], in0=CA[:, cur, :, 0:129],
            in1=M0B[:, :].unsqueeze(2).to_broadcast([128, 4, 129]), op=ALU.mult)
        k.dma(SP, oC[j].rearrange("h a b -> a h b"), STG[:, :, 0:128], reads=[dSTG])
        store_T(on[j], STG[:, :, 128], 4, [dSTG])
        k.dma(SP, oS[j].rearrange("h a b -> a h b"), SS[:, cur].bitcast(F32), reads=[dSS[cur]])
        dve("tensor_copy", dHIST[seg.sidx], [dHS], out=HS[:, :].rearrange("p (t c) -> p t c", t=3),
            in_=HIST4[:, seg.sidx].rearrange("p c t -> p t c"))
        store_T(ocv[j].rearrange("t (c p) -> (t c) p", p=128), HS[:, :], 36, [dHS])

    PTF = [PT[0][:].bitcast(F32), PT[1][:].bitcast(F32)]
    pre_out = {}

    def prelude_gen(i, t):
        L, col = t.L, t.col
        par = i % 2
        SMt = SM[:, 128 * par:128 * par + 128]
        dSg_ = dSgp[par]
        S_ = [dSg_]
        smc = lambda a_, b_, L_: SMt[:L_, a_:b_]
        G1 = smc(0, 16, L)
        EX = smc(16, 28, L)
        L2 = smc(28, 36, L)
        LG = smc(36, 44, L)
        BETA = smc(44, 48, L)
        NBETA = smc(48, 52, L)
        CS = smc(52, 60, L)
        A_ = smc(60, 64, L)
        GAM = smc(64, 68, L)
        DL = smc(68, 72, L)
        WS = smc(72, 76, L)
        DEN = smc(76, 80, L)
        RINV = smc(80, 84, L)
        SSQ = smc(84, 88, L)
        SCL = smc(88, 92, L)
        TMP4 = smc(92, 96, L)
        SSQ2 = smc(96, 100, L)
        SCL2 = smc(100, 104, L)
        NG = smc(104, 108, L)
        dve("tensor_tensor", [dGT[i], dconst], S_, out=G1, in0=GT[:L, i, :], in1=GB[:L, :], op=ALU.add)
        yield
        dve("tensor_tensor", S_ + [dconst], S_, out=G1, in0=G1, in1=SGN[:L, :], op=ALU.mult)
        yield
        k.act(EX, SMt[:L, 4:16], AF.Exp, S_, S_)
        yield
        k.act(L2, SMt[:L, 16:24], AF.Ln, S_, S_, bias=1.0)
        yield
        dve("tensor_scalar_add", S_, S_, out=BETA, in0=SMt[:L, 24:28], scalar1=1.0)
        yield
        dve("reciprocal", S_, S_, out=BETA, in_=BETA)
        yield
        dve("tensor_scalar_mul", S_, S_, out=NBETA, in0=BETA, scalar1=-1.0)
        yield
        dve("tensor_tensor", S_ + [dconst], S_, out=LG, in0=L2, in1=NEG8[:L, :], op=ALU.mult)
        yield
        p, dp = PTF[0], ptd[0]
        k.mm(p[:L, 0:8], tri[:L, :L], LG, True, True, S_ + [dconst], [dp])
        yield
        dve("tensor_copy", [dp], S_, out=CS, in_=p[:L, 0:8])
        yield
        dve("tensor_tensor", S_ + [dconst], [dRBd], out=RB[:L, 0:4, :L],
            in0=tri[:L, :L].unsqueeze(1).to_broadcast([L, 4, L]),
            in1=SMt[:L, 36:40].unsqueeze(2).to_broadcast([L, 4, L]), op=ALU.mult)
        yield
        pool("tensor_tensor", S_ + [dconst], [dRBg], out=RB[:L, 4:8, :L],
             in0=tri[:L, :L].unsqueeze(1).to_broadcast([L, 4, L]),
             in1=SMt[:L, 40:44].unsqueeze(2).to_broadcast([L, 4, L]), op=ALU.mult)
        yield
        pb, dpb = PTF[1], ptd[1]
        pg, dpg = PTF[0], ptd[0]
        pbv = pb[:, 0:4 * L].rearrange("p (h t) -> p h t", h=4)
        pgv = pg[:, 0:4 * L].rearrange("p (h t) -> p h t", h=4)
        k.mm(pbv, ones[:L, :], RB[:L, 0:4, :L], True, True, [dRBd, dconst], [dpb])
        yield
        k.mm(pgv, ones[:L, :], RB[:L, 4:8, :L], True, True, [dRBg, dconst], [dpg])
        yield
        dve("tensor_tensor", S_, S_, out=A_, in0=SMt[:L, 0:4], in1=SMt[:L, 52:56], op=ALU.subtract)
        yield
        dve("tensor_scalar_mul", S_, S_, out=NG, in0=SMt[:L, 56:60], scalar1=-1.0)


        yield
        pre_out[i] = (pbv, dpb, pgv, dpg)
        yield

    def mixer_tile(i, t, nxt_t=None):
        L, col = t.L, t.col
        par = i % 2
        cur = state["cur"]
        nxt = 1 - cur
        SMt = SM[:, 128 * par:128 * par + 128]
        dSg_ = dSgp[par]
        BLGt, dBLGt = BLG2[par], dBLGp[par]
        S_ = [dSg_]
        smc = lambda a_, b_, L_: SMt[:L_, a_:b_]
        if i not in pre_out:
            for _ in prelude_gen(i, t):
                pass
        pbv, dpb, pgv, dpg = pre_out.pop(i)
        G1 = smc(0, 16, L)
        EX = smc(16, 28, L)
        L2 = smc(28, 36, L)
        LG = smc(36, 44, L)
        BETA = smc(44, 48, L)
        NBETA = smc(48, 52, L)
        CS = smc(52, 60, L)
        A_ = smc(60, 64, L)
        GAM = smc(64, 68, L)
        DL = smc(68, 72, L)
        WS = smc(72, 76, L)
        DEN = smc(76, 80, L)
        RINV = smc(80, 84, L)
        SSQ = smc(84, 88, L)
        SCL = smc(88, 92, L)
        TMP4 = smc(92, 96, L)
        SSQ2 = smc(96, 100, L)
        SCL2 = smc(100, 104, L)
        NG = smc(104, 108, L)
        dve("tensor_copy", [dpb], [dBLGt], out=BLGt[:, 0:4], in_=pbv[:, :, L - 1])
        dve("tensor_copy", [dpg], [dBLGt], out=BLGt[:, 4:8], in_=pgv[:, :, L - 1])
        EG, dEG = TT[2], dTT[2]
        EGv = EG[:, 0:4 * L].rearrange("p (h t) -> p h t", h=4)
        k.act(EGv, pgv, AF.Exp, [dpg], [dEG])
        DT_, dDT = TT[3], dTT[3]
        DTv = DT_[:L, 0:4 * L].rearrange("p (h t) -> p h t", h=4)
        for h in range(4):
            dve("tensor_scalar", [dpg] + S_, [dDT], out=DTv[:, h, :], in0=pgv[:L, h, :], scalar1=SMt[:L, 104 + h:105 + h],
                scalar2=0.0, op0=ALU.add, op1=ALU.min)
        def gen_a():
            Sr = [dSg_, dSa]
            S_ = [dSa]
            E1, dE1 = TT[0], dTT[0]
            E1v = E1[:, 0:4 * L].rearrange("p (h t) -> p h t", h=4)
            k.act(E1v, pbv, AF.Exp, [dpb], [dE1])
            yield
            WT, dWT = TT[1], dTT[1]
            WTv = WT[:L, 0:4 * L].rearrange("p (h t) -> p h t", h=4)
            for h in range(4):
                k.act(WTv[:, h, :], pbv[:L, h, :], AF.Exp, [dpb] + Sr, [dWT], bias=SMt[:L, 60 + h:61 + h])
            yield
            dve("tensor_tensor", [dWT, dconst], [dWT], out=WTv, in0=WTv,
                in1=maskA[:L, :L].unsqueeze(1).to_broadcast([L, 4, L]), op=ALU.mult)
            yield
            QS, dQS = QSR, dQSR
            QSv = QS[:, 0:4 * L].rearrange("p (h t) -> p h t", h=4)
            dve("tensor_tensor", dQKA[0:4] + [dE1], [dQS], out=QSv, in0=QKA[:, 0:4, col:col + L], in1=E1v, op=ALU.mult)
            yield
            pst, dpst = psA()
            pstv = pst[:L, 0:4 * L].rearrange("p (h t) -> p h t", h=4)
            for h in range(4):
                k.mm(pstv[:, h, :], QKA[:, 4 + h, col:col + L], QKA[:, h, col:col + L], True, True,
                     [dQKA[h], dQKA[4 + h]], [dpst], inc=(h == 3))
            yield
            SMK, dSMK = SMKR, dSMKR
            SMKv = SMK[:L, 0:4 * L].rearrange("p (h t) -> p h t", h=4)
            dve("tensor_tensor", [dpst, dWT], [dSMK], out=SMKv, in0=pstv, in1=WTv, op=ALU.mult)
            yield
            dve("tensor_tensor", Sr + [dBLGt], S_, out=WS, in0=A_, in1=BLGt[:L, 0:4], op=ALU.add)
            yield
            k.act(WS, WS, AF.Exp, Sr, S_, bias=-0.5 * float(np.log(128.0)))
            yield
            dKW = dKWd
            pool("tensor_tensor", [dKA[i]] + Sr, [dKW], out=KW[:L, :, :],
                 in0=KA[:L, i, :].rearrange("p (h d) -> p h d", h=4),
                 in1=WS.unsqueeze(2).to_broadcast([L, 4, 128]), op=ALU.mult)
            yield
            pn_ = []
            for half in range(2):
                p, dp = psA()
                pv = p[:L, 0:260].rearrange("p (h d) -> p h d", h=2)
                for hh in range(2):
                    h = half * 2 + hh
                    k.mm(pv[:, hh, :], QSv[:, h, :], CA[:, cur, h, :], True, False, [dQS, dCA[cur]], [dp], inc=False)
                    k.mm(pv[:, hh, :], SMKv[:, h, :], VA[:L, i, h, :], False, True, [dSMK, dVA[i]], [dp], inc=True)
                pn_.append((pv, dp))
            yield
            for half in range(2):
                pv, dp = pn_[half]
                dve("tensor_copy", [dp], S_, out=SMt[:L, 76 + 2 * half:78 + 2 * half], in_=pv[:, :, 128])
                for hh in range(2):
                    h = half * 2 + hh
                    k.act(JUNK[:L, 0:128], pv[:, hh, 0:128], AF.Square, [dp], S_, accum=SMt[:L, 84 + h:85 + h])
            yield
            dve("scalar_tensor_tensor", Sr, S_, out=RINV, in0=DEN, scalar=-1.0, in1=DEN, op0=ALU.mult, op1=ALU.max)
            yield
            dve("tensor_scalar_max", Sr, S_, out=RINV, in0=RINV, scalar1=1.0)
            yield
            dve("reciprocal", Sr, S_, out=RINV, in_=RINV)
            yield
            dve("tensor_tensor", Sr, S_, out=TMP4, in0=RINV, in1=RINV, op=ALU.mult)
            yield
            dve("tensor_tensor", Sr, S_, out=TMP4, in0=TMP4, in1=SSQ, op=ALU.mult)
            yield
            k.act(TMP4, TMP4, AF.Ln, Sr, S_, bias=EPS, scale=1.0 / 128)
            yield
            k.act(TMP4, TMP4, AF.Exp, Sr, S_, scale=-0.5)
            yield
            dve("tensor_tensor", Sr, S_, out=SCL, in0=TMP4, in1=RINV, op=ALU.mult)
            yield
            for half in range(2):
                pv, dp = pn_[half]
                for hh in range(2):
                    h = half * 2 + hh
                    dve("scalar_tensor_tensor", [dp, dOG[i]] + Sr, [dH[0]], out=H[:L, 0, h * 128:(h + 1) * 128],
                        in0=pv[:, hh, 0:128], scalar=SMt[:L, 88 + h:89 + h], in1=OG[:L, i, h * 128:(h + 1) * 128],
                        op0=ALU.mult, op1=ALU.mult)
            yield
            for half in range(2):
                p, dp = psA()
                pv = p[:, 0:260].rearrange("p (h d) -> p h d", h=2)
                for hh in range(2):
                    h = half * 2 + hh
                    k.mm(pv[:, hh, :], KW[:L, h, :], VA[:L, i, h, :], True, True, [dKW, dVA[i]], [dp], inc=(hh == 1))
                for hh in range(2):
                    h = half * 2 + hh
                    dve("scalar_tensor_tensor", [dp, dCA[cur], dE1], [dCA[nxt]], out=CA[:, nxt, h, :], in0=CA[:, cur, h, :],
                        scalar=E1v[:, h, L - 1:L], in1=pv[:, hh, :], op0=ALU.mult, op1=ALU.add)
            yield
            dve("tensor_tensor", Sr + [dBBC], [dAGd], out=AG[:L, 0:4], in0=A_, in1=BBC[:L, :], op=ALU.subtract)
            yield
            dve("tensor_copy", Sr, [dAGd], out=AG[:L, 4:8], in_=SMt[:L, 36:40])
            yield
            p, dp = psA()
            k.tr(p[0:4, 0:L], AG[:L, 0:4], ident[:L, :L], [dAGd, dconst], [dp], inc=False)
            k.tr(p[0:4, 128:128 + L], AG[:L, 4:8], ident[:L, :L], [dAGd, dconst], [dp])
            dve("reduce_max", [dp], [dM4], out=M4[:, 2:3], in_=p[0:4, 0:L], axis=AX.X)
            yield
            dve("tensor_tensor", [dM4], [dM4], out=M4[:, 0:1], in0=M4[:, 0:1], in1=M4[:, 2:3], op=ALU.max)
            yield
            dve("reduce_sum", [dp], [dM4], out=M4[:, 3:4], in_=p[0:4, 128:128 + L], axis=AX.X)
            yield
            dve("tensor_tensor", [dM4], [dM4], out=M4[:, 1:2], in0=M4[:, 1:2], in1=M4[:, 3:4], op=ALU.add)
            yield
            dve("tensor_tensor", [dBBC, dBLGt], [dBBC], out=BBC[:, :], in0=BBC[:, :], in1=BLGt[:, 0:4], op=ALU.add)


            yield
        def gen_b():
            Sr = [dSg_, dSb]
            S_ = [dSb]
            k.act(DTv, DTv, AF.Exp, [dDT], [dDT])
            yield
            DMS, dDMS = TT[4], dTT[4]
            DMSv = DMS[:L, 0:4 * L].rearrange("p (h t) -> p h t", h=4)
            dve("tensor_tensor", [dDT, dconst], [dDMS], out=DMSv, in0=DTv,
                in1=mstrict[:L, :L].unsqueeze(1).to_broadcast([L, 4, L]), op=ALU.mult)
            yield
            pool("tensor_tensor", [dDT, dconst], [dDT], out=DTv, in0=DTv,
                 in1=tri[:L, :L].unsqueeze(1).to_broadcast([L, 4, L]), op=ALU.mult)
            yield
            k.act(GAM, SMt[:L, 56:60], AF.Exp, Sr, S_)
            yield
            dve("tensor_tensor", Sr + [dBLGt], S_, out=DL, in0=NG, in1=BLGt[:L, 4:8], op=ALU.add)
            yield
            k.act(DL, DL, AF.Exp, Sr, S_)
            yield
            pk, dpk = psB()
            pkv = pk[:L, :].rearrange("p (h d) -> p h d", h=4)
            for h in range(4):
                k.tr(pkv[:, h, :], POSTF[:, 4 + h, col:col + L], ident[:, :], [dPOST[4 + h], dconst], [dpk], inc=(h == 3))
            yield
            pool_or_dve = dve
            dve("tensor_tensor", [dpk] + Sr, [dGKd], out=GK[:L, :, :], in0=pkv, in1=GAM.unsqueeze(2).to_broadcast([L, 4, 128]),
                op=ALU.mult)
            yield
            dve("tensor_tensor", [dpk] + Sr, [dKDd], out=KD[:L, :, :], in0=pkv, in1=DL.unsqueeze(2).to_broadcast([L, 4, 128]),
                op=ALU.mult)
            yield
            pv_, dpv = psB()
            pvv = pv_[:L, :].rearrange("p (h d) -> p h d", h=4)
            for h in range(4):
                k.tr(pvv[:, h, :], POSTF[:, 8 + h, col:col + L], ident[:, :], [dPOST[8 + h], dconst], [dpv], inc=(h == 3))
            yield
            k.act(VB[:L, :, :], pvv, AF.Copy, [dpv], [dVBd])
            yield
            pkk, dpkk = psB()
            pkkv = pkk[:L, 0:4 * L].rearrange("p (h t) -> p h t", h=4)
            for h in range(4):
                k.mm(pkkv[:, h, :], POST[:, 4 + h, col:col + L], POST[:, 4 + h, col:col + L], True, True, [dPOST[4 + h]],
                     [dpkk], inc=(h == 3))
            yield
            ny = 0
            for h in range(4):
                dve("scalar_tensor_tensor", [dpkk, dDMS] + Sr, [dNYFh[h // 2]], out=NYF[:L, h, :L], in0=pkkv[:, h, :],
                    scalar=SMt[:L, 48 + h:49 + h], in1=DMSv[:, h, :], op0=ALU.mult, op1=ALU.mult)
            yield
            pqk, dpqk = psB()
            pqkv = pqk[:L, 0:4 * L].rearrange("p (h t) -> p h t", h=4)
            for h in range(4):
                k.mm(pqkv[:, h, :], POST[:, 4 + h, col:col + L], POST[:, h, col:col + L], True, True,
                     [dPOST[4 + h], dPOST[h]], [dpqk], inc=(h == 3))
            yield
            QKM, dQKM = QKMR, dQKMR
            QKMv = QKMR[:L, 0:4 * L].rearrange("p (h t) -> p h t", h=4)
            dve("tensor_tensor", [dpqk, dDT], [dQKM], out=QKMv, in0=pqkv, in1=DTv, op=ALU.mult)
            yield
            QG, dQG = QGR, dQGR
            QGv = QGR[:, 0:4 * L].rearrange("p (h t) -> p h t", h=4)
            dve("tensor_tensor", dPOST[0:4] + [dEG, dQKM], [dQG], out=QGv, in0=POST[:, 0:4, col:col + L], in1=EGv, op=ALU.mult)
            yield
            npi = 0
            nlev = int(np.log2(L)) - 1

            def half_chain(hf):
                h0 = 2 * hf
                hs = (h0, h0 + 1)
                dY, dZ, dP = dNYFh[hf], dNZFh[hf], dNPh[hf]
                p, dp = psB()
                pzv = p[:L, 0:2 * L].rearrange("p (h t) -> p h t", h=2)
                for a, h in enumerate(hs):
                    k.tr(pzv[:, a, :], NYF[:L, h, :L].bitcast(F32), ident[:L, :L], [dY, dconst], [dp], inc=(a == 1))
                yield
                k.act(NZF[:L, 0, h0:h0 + 2, :L], pzv, AF.Copy, [dp], [dZ[0]])
                yield
                pool("tensor_tensor", [dY, dconst], [dP], out=NP_[:L, npi, h0:h0 + 2, :L], in0=NYF[:L, h0:h0 + 2, :L],
                     in1=ident[:L, :L].unsqueeze(1).to_broadcast([L, 2, L]), op=ALU.add)
                yield
                zf = 0
                for lev in range(nlev):
                    last = (lev == nlev - 1)
                    p, dp = psB()
                    pzv = p[:L, 0:2 * L].rearrange("p (h t) -> p h t", h=2)
                    for a, h in enumerate(hs):
                        k.mm(pzv[:, a, :], NYF[:L, h, :L], NZF[:L, zf, h, :L], True, True, [dY, dZ[zf]], [dp], inc=(a == 1))
                    yield
                    k.act(NZF[:L, 1 - zf, h0:h0 + 2, :L], pzv, AF.Copy, [dp], [dZ[1 - zf]])
                    yield
                    if not last:
                        p2, dp2 = psB()
                        pyv = p2[:L, 0:2 * L].rearrange("p (h t) -> p h t", h=2)
                        for a, h in enumerate(hs):
                            k.mm(pyv[:, a, :], NZF[:L, zf, h, :L], NYF[:L, h, :L], True, True, [dY, dZ[zf]], [dp2],
                                 inc=(a == 1))
                        yield
                        dve("tensor_copy", [dp2], [dY], out=NYF[:L, h0:h0 + 2, :L], in_=pyv)
                        yield
                    p3, dp3 = psB()
                    ppv = p3[:L, 0:2 * L].rearrange("p (h t) -> p h t", h=2)
                    for a, h in enumerate(hs):
                        k.mm(ppv[:, a, :], NZF[:L, 1 - zf, h, :L], NP_[:L, npi, h, :L], True, True, [dZ[1 - zf], dP], [dp3],
                             inc=(a == 1))
                    yield
                    zf = 1 - zf
                    dve("tensor_tensor", [dp3, dP], [dP], out=NP_[:L, npi, h0:h0 + 2, :L], in0=ppv,
                        in1=NP_[:L, npi, h0:h0 + 2, :L], op=ALU.add)
                    yield

            subs = [half_chain(0), half_chain(1)]
            while subs:
                for s_ in list(subs):
                    try:
                        next(s_)
                    except StopIteration:
                        subs.remove(s_)
                yield
            Q_ = NP_[:L, npi]
            dQ = dNPh[0]
            dQ2 = dNPh[1]
            p, dp = psB()
            puv = p[:L, :].rearrange("p (h d) -> p h d", h=4)
            for h in range(4):
                k.mm(puv[:, h, :], Q_[:, h, :L], VB[:L, h, :], True, True, [dQ, dQ2, dVBd], [dp], inc=(h == 3))
            yield
            dve("tensor_tensor", [dp] + Sr, [dBUd], out=BU[:L, :, :], in0=puv, in1=BETA.unsqueeze(2).to_broadcast([L, 4, 128]),
                op=ALU.mult)
            yield
            p, dp = psB()
            pwv = p[:, 0:4 * L].rearrange("p (h t) -> p h t", h=4)
            for h in range(4):
                k.mm(pwv[:, h, :], GK[:L, h, :], Q_[:, h, :L], True, True, [dQ, dQ2, dGKd], [dp], inc=(h == 3))
            yield
            k.act(WKT[:, :, :L], pwv, AF.Copy, [dp], [dWKTd])
            yield
            p, dp = psB()
            p1v = p[:L, :].rearrange("p (h d) -> p h d", h=4)
            for h in range(4):
                k.mm(p1v[:, h, :], WKT[:, h, :L], SS[:, cur, h, :], True, True, [dWKTd, dSS[cur]], [dp], inc=(h == 3))
            yield
            WV = BU
            for h in range(4):
                dve("scalar_tensor_tensor", [dp, dBUd] + Sr, [dWVd], out=BU[:L, h, :], in0=p1v[:, h, :],
                    scalar=SMt[:L, 48 + h:49 + h], in1=BU[:L, h, :], op0=ALU.mult, op1=ALU.add)
            yield
            po, dpo = psB()
            pov = po[:L, :].rearrange("p (h d) -> p h d", h=4)
            for h in range(4):
                k.mm(pov[:, h, :], QGv[:, h, :], SS[:, cur, h, :], True, False, [dQG, dSS[cur]], [dpo], inc=False)
                k.mm(pov[:, h, :], QKMv[:, h, :], WV[:L, h, :], False, True, [dQKM, dWVd], [dpo], inc=True)
            yield
            p, dp = psB()
            psv = p[:, :].rearrange("p (h d) -> p h d", h=4)
            for h in range(4):
                k.mm(psv[:, h, :], KD[:L, h, :], WV[:L, h, :], True, True, [dKDd, dWVd], [dp], inc=(h == 3))
            yield
            dve("tensor_tensor", [dSS[cur], dEG], [dSS[nxt]], out=SS[:, nxt], in0=SS[:, cur],
                in1=EGv[:, :, L - 1:L].to_broadcast([128, 4, 128]), op=ALU.mult)
            yield
            dve("tensor_tensor", [dSS[nxt], dp], [dSS[nxt]], out=SS[:, nxt], in0=SS[:, nxt], in1=psv, op=ALU.add)
            yield
            for h in range(4):
                k.act(JUNK[:L, 0:128], pov[:, h, :], AF.Square, [dpo], S_, accum=SMt[:L, 96 + h:97 + h])
            yield
            k.act(SSQ2, SSQ2, AF.Ln, Sr, S_, bias=EPS, scale=1.0 / 128)
            yield
            k.act(SCL2, SSQ2, AF.Exp, Sr, S_, scale=-0.5)
            yield
            for h in range(4):
                dve("scalar_tensor_tensor", [dpo, dZG[i]] + Sr, [dH[0]], out=H[:L, 0, 512 + h * 128:512 + (h + 1) * 128],
                    in0=pov[:, h, :], scalar=SMt[:L, 100 + h:101 + h], in1=ZG[:L, i, h * 128:(h + 1) * 128],
                    op0=ALU.mult, op1=ALU.mult)

            yield
        gens_ = [gen_a(), gen_b()]
        if nxt_t is not None:
            gens_.append(prelude_gen(i + 1, nxt_t))
        interleave(gens_, 3)
        state["cur"] = nxt
        p, dp = PS[0][:].bitcast(BF16), psd[0]
        pv = p.rearrange("p (k c) -> p k c", k=8)
        for kc in range(8):
            k.tr(pv[:, kc, :L], H[:L, 0, kc * 128:(kc + 1) * 128], identb[:L, :L], [dH[0], dconst], [dp], inc=(kc == 7))
        k.act(HT[:, :, col:col + L], pv[:, :, :L], AF.Copy, [dp], [dHT])

    dRBd, dKWd, dAGd, dGKd, dKDd, dVBd, dBUd, dWKTd, dWVd, dDG, dSTG, dM0B, dBLG = [Dep() for _ in range(13)]
    dNY = [Dep(), Dep()]
    dNZ = [Dep(), Dep()]
    dNP = [Dep(), Dep()]

    def add_to_x(p, dp, i, L, cb):
        dve("tensor_tensor", [dp, dX[i]], [dX[i]], out=X[:L, i, cb * 512:(cb + 1) * 512], in0=X[:L, i, cb * 512:(cb + 1) * 512],
            in1=p[:L, :], op=ALU.add)

    def proj_resid(name, tiles, src, dsrc, norm_idx):
        ws = [wblk(name, 0), wblk(name, 1, ahead=1)]
        for i, t in enumerate(tiles):
            for cb in range(2):
                w, dw = ws[cb]
                p, dp = ps()
                for kc in range(8):
                    k.mm(p[:t.L, :], src[:, kc, t.col:t.col + t.L], w[:, kc, :], kc == 0, kc == 7, [dsrc, dw], [dp])
                add_to_x(p, dp, i, t.L, cb)
            if i >= 1:
                ti = tiles[i - 1]
                rmsnorm_to_xnt(X[:ti.L, i - 1, :], dX[i - 1], ti.L, ti.col, norm_idx, (i - 1) % 2)
        ti = tiles[-1]
        rmsnorm_to_xnt(X[:ti.L, len(tiles) - 1, :], dX[len(tiles) - 1], ti.L, ti.col, norm_idx, (len(tiles) - 1) % 2)

    dQF, dPTS, dOTB, dPF, dPN, dPF2, dPN2 = [Dep() for _ in range(7)]
    dSat = [Dep() for _ in range(4)]
    PF2 = ARM[:, 0:1024]
    PN2 = ARM[:, 1024:1536].bitcast(BF16)

    def phase_attn(tiles, segs, NT):
        for cb in range(2):
            w, dw = wblk("q", cb)
            for jj in range(4):
                p, dp = proj_fm(w, dw, jj, NT)
                k.act(QF[:, cb * 4 + jj, :NT], p[:, :NT], AF.Copy, [dp], [dQF] + dQKA[0:4])
        def attn_tile(i, t):
            L, col = t.L, t.col
            par = i % 2
            PFt, PNt, dPFt, dPNt = (PF, PN, dPF, dPN) if par == 0 else (PF2, PN2, dPF2, dPN2)
            c0 = 128 + 32 * i
            S_ = [dSat[i]]
            while len(free_ps) < 2:
                yield
            bks = [free_ps.pop(0), free_ps.pop(0)]
            pp = []
            for half in range(2):
                p, dp = PS[bks[half]], psd[bks[half]]
                pv = p[:L, :].rearrange("p (h n) -> p h n", h=2)
                for hh in range(2):
                    h = half * 2 + hh
                    for dc in range(2):
                        k.mm(pv[:, hh, :], QF[:, 2 * h + dc, col:col + L], KT[:, 2 * h + dc, :], dc == 0, dc == 1,
                             [dQF, dKT], [dp], inc=(hh == 1 and dc == 1))
                pp.append((pv, dp))
            yield
            for half in range(2):
                pv, dp = pp[half]
                dve("reduce_max", [dp], S_, out=SM[:L, c0 + 2 * half:c0 + 2 + 2 * half], in_=pv, axis=AX.X)
            dve("tensor_scalar_mul", S_, S_, out=SM[:L, c0 + 4:c0 + 8], in0=SM[:L, c0:c0 + 4], scalar1=-1.0 / 16)
            yield
            PFv = PFt[:L, :].rearrange("p (h n) -> p h n", h=4)
            for half in range(2):
                pv, dp = pp[half]
                for hh in range(2):
                    h = half * 2 + hh
                    k.act(PFv[:, h, :], pv[:, hh, :], AF.Exp, [dp] + S_, [dPFt] + (dOG if par == 0 else dPOST[0:2]),
                          bias=SM[:L, c0 + 4 + h:c0 + 5 + h],
                          scale=1.0 / 16, accum=SM[:L, c0 + 8 + h:c0 + 9 + h])
            free_ps.extend(bks)
            yield
            dve("reciprocal", S_ + [dPFt], S_, out=SM[:L, c0 + 12:c0 + 16], in_=SM[:L, c0 + 8:c0 + 12])
            dve("tensor_tensor", [dPFt] + S_, [dPNt] + (dZG if par == 0 else dPOST[2:3]),
                out=PNt[:L, :].rearrange("p (h n) -> p h n", h=4), in0=PFv,
                in1=SM[:L, c0 + 12:c0 + 16].unsqueeze(2).to_broadcast([L, 4, 256]), op=ALU.mult)
            yield
            p, dp = pt()
            pv = p[:].rearrange("p (k c) -> p k c", k=8)
            for c8 in range(8):
                k.tr(pv[:, c8, :L], PNt[:L, c8 * 128:(c8 + 1) * 128], identb[:L, :L], [dPNt, dconst], [dp], inc=(c8 == 7))
            yield
            k.act(PTS[:, :, col:col + L], pv[:, :, :L], AF.Copy, [dp], [dPTS] + dQKA[4:8])
            yield

        for seg in segs:
            if seg.kind == "s":
                sample_kv(seg.j)
            interleave([attn_tile(i, t) for i, t in enumerate(tiles) if t.seg is seg], 2)
            assert len(free_ps) == 6
            for h in range(4):
                for dc in range(2):
                    p, dp = ps()
                    for ncx in range(2):
                        k.mm(p[:, :seg.n], VV[:, ncx, h * 256 + dc * 128:h * 256 + (dc + 1) * 128],
                             PTS[:, h * 2 + ncx, seg.col0:seg.col0 + seg.n], ncx == 0, ncx == 1, [dVV, dPTS], [dp])
                    k.act(OTB[:, h * 2 + dc, seg.col0:seg.col0 + seg.n], p[:, :seg.n], AF.Copy, [dp], [dOTB] + dKA)
        proj_resid("o", tiles, OTB, dOTB, 2)

    dHID = [Dep() for _ in range(32)]

    def phase_ffn(tiles, NT, next_tiles=None):
        if next_tiles is not None:
            prefetch_x(next_tiles)
        for b in range(8):
            w, dw = wblk("1", b)
            for jj in range(4):
                f = b * 4 + jj
                p, dp = proj_fm(w, dw, jj, NT)
                tt, dtt = tmp()
                k.act(tt[:, :NT], p[:, :NT], AF.Relu, [dp], [dtt])
                pool("tensor_tensor", [dtt], [dHID[f]], out=HID[:, f, :NT], in0=tt[:, :NT], in1=tt[:, :NT], op=ALU.mult)
        for cb in range(2):
            acc = [ps() for _ in tiles]
            for fb in range(4):
                w, dw = wblk("2", cb * 4 + fb)
                for i, t in enumerate(tiles):
                    p, dp = acc[i]
                    for f8 in range(8):
                        f = fb * 8 + f8
                        k.mm(p[:t.L, :], HID[:, f, t.col:t.col + t.L], w[:, f8, :], fb == 0 and f8 == 0,
                             fb == 3 and f8 == 7, [dHID[f], dw], [dp], inc=(f8 == 7))
                if next_tiles is not None:
                    blk_i = cb * 4 + fb
                    al = lambda i_: [dXP[i_]] + dKA + dOG + dZG
                    if blk_i == 0:
                        for i_ in (0, 1):
                            norm_pre(XP[:128, i_, :], dXP[i_], 128, i_ % 2, al(i_))
                    elif blk_i == 1:
                        for i_ in (0, 1):
                            norm_tr(128, next_tiles[i_].col, 0, i_ % 2)
                        for i_ in (2, 3):
                            norm_pre(XP[:128, i_, :], dXP[i_], 128, i_ % 2, al(i_))
                    elif blk_i == 2:
                        for i_ in (2, 3):
                            norm_tr(128, next_tiles[i_].col, 0, i_ % 2)
            for i, t in enumerate(tiles):
                p, dp = acc[i]
                add_to_x(p, dp, i, t.L, cb)
        for i, t in enumerate(tiles):
            L = t.L
            par = i % 2
            d_ = dSTF[par]
            k.act(JUNK[:L, :], X[:L, i, :], AF.Square, [dX[i]], [d_], accum=ST[:L, 16 + par:17 + par])
            k.act(ST[:L, 18 + par:19 + par], ST[:L, 16 + par:17 + par], AF.Ln, [d_], [d_], bias=EPS, scale=1.0 / D)
            k.act(ST[:L, 20 + par:21 + par], ST[:L, 18 + par:19 + par], AF.Exp, [d_], [d_], scale=-0.5)
            dve("scalar_tensor_tensor", [dX[i], d_, dconst], [dX[i]], out=X[:L, i, :], in0=X[:L, i, :],
                scalar=ST[:L, 20 + par:21 + par], in1=WBF[:L, :], op0=ALU.mult, op1=ALU.mult)
            k.dma(SP, t.dst, X[:L, i, :], reads=[dX[i]])

    def run_group(tiles, segs, NT, pref=False, next_tiles=None):
        for seg in segs:
            if seg.first:
                hist_init(seg)
        phase_in(tiles, segs, NT, pref)
        for i, t in enumerate(tiles):
            seg = t.seg
            if seg.first and t.col == seg.col0:
                state_init(seg)
            mixer_tile(i, t, tiles[i + 1] if i + 1 < len(tiles) else None)
            if seg.last and t.col + t.L == seg.col0 + seg.n:
                seq_final(seg)
        proj_resid("out", tiles, HT, dHT, 1)
        k.barrier()
        phase_attn(tiles, segs, NT)
        k.barrier()
        phase_ffn(tiles, NT, next_tiles)
        k.barrier()

    if stage == 0:
        pass
    elif stage < 2:
        mem_kv(0)
        mem_kv(1)
    else:
        tiles, segs = [], []
        for j in range(NSS):
            seg = Seg("s", j, j * LS, LS, True, True, sidx=j)
            segs.append(seg)
            tiles.append(Tile(LS, j * LS, xs[j], ys[j], seg))
        run_group(tiles, segs, NSS * LS)
        for j in range(NPS):
            mem_kv(j)
            def mk(g):
                seg = Seg("p", j, 0, 512, g == 0, g == KNG - 1, sidx=0)
                return seg, [Tile(128, i * 128, xp[j, g * 512 + i * 128:g * 512 + (i + 1) * 128, :],
                                  yp[j, g * 512 + i * 128:g * 512 + (i + 1) * 128, :], seg) for i in range(4)]
            for g in range(KNG):
                seg, tiles = mk(g)
                nxt_tiles = mk(g + 1)[1] if g + 1 < KNG else None
                run_group(tiles, [seg], 512, pref=(g > 0), next_tiles=nxt_tiles)
    print("emitted instructions:", k.nins, "waits:", k.nwait, "dmas:", k.dma_i)

    for i in range(ND):
        if k.dcnt[i] > 0:
            k.wait(SP, (k.dsem[i], k.dcnt[i]))
    for i in range(16):
        if k.wcnt[i] > 0:
            k.wait(SP, (k.wsem[i], k.wcnt[i]))
    for E in k.engs:
        if E.cnt > 0:
            k.wait(SP, (E.sem, E.cnt))
    k.es.close()
    return k


def _shard(inputs):
    f = lambda a: np.ascontiguousarray(np.asarray(a, dtype=np.float32))
    maps = []
    for c in range(NCORES):
        p0, s0 = c * NPS, c * NSS
        m = {
            "xp": f(inputs["x_prompt"][p0:p0 + NPS]),
            "xs": f(inputs["x_sample"][s0:s0 + NSS]),
            "sC": f(inputs["state_mlstm_C"][0, s0:s0 + NSS]),
            "sn": f(inputs["state_mlstm_n"][0, s0:s0 + NSS]),
            "sm": f(inputs["state_mlstm_m"][0, s0:s0 + NSS]),
            "sS": f(inputs["state_gdn_S"][0, s0:s0 + NSS]),
            "scv": f(inputs["state_gdn_conv"][0, s0:s0 + NSS]),
            "ck": f(np.asarray(inputs["cache_mem_k"])[0, s0:s0 + NSS].reshape(NSS, NMEM, D)),
            "cv": f(np.asarray(inputs["cache_mem_v"])[0, s0:s0 + NSS].reshape(NSS, NMEM, D)),
            "memp": f(inputs["mem_prompt"][p0:p0 + NPS]),
            "w_in": f(inputs["w_in"][0]), "w_out": f(inputs["w_out"][0]),
            "wq": f(inputs["wq_x"][0]), "wk": f(inputs["wk_x"][0]), "wv": f(inputs["wv_x"][0]),
            "wo": f(inputs["wo_x"][0]), "w1": f(inputs["w_ff1"][0]), "w2": f(inputs["w_ff2"][0]),
            "nw_mix": f(inputs["norm_mix_w"][0]), "nw_x": f(inputs["norm_x_w"][0]),
            "nw_mem": f(inputs["norm_mem_w"][0]), "nw_ffn": f(inputs["norm_ffn_w"][0]),
            "nw_fin": f(inputs["norm_final_w"]),
            "b_ig": f(inputs["mlstm_igate_b"][0]), "b_fg": f(inputs["mlstm_fgate_b"][0]),
            "nw_a": f(inputs["mlstm_norm_w"][0]), "nw_b": f(inputs["gdn_norm_w"][0]),
            "convw": f(inputs["gdn_conv_w"][0]), "a_log": f(inputs["gdn_A_log"][0]), "dt_b": f(inputs["gdn_dt_bias"][0]),
        }
        maps.append(m)
    return maps


_CACHE = {}


def kernel(**inputs):
    stage = int(os.environ.get("KSTAGE", "9"))
    if stage not in _CACHE:
        _CACHE[stage] = build(stage)
    kb = _CACHE[stage]
    maps = _shard(inputs)
    res = run_bass_kernel_spmd(kb.nc, maps, core_ids=list(range(NCORES)))
    R = res.results
    cat = lambda nm: np.concatenate([np.asarray(r[nm], dtype=np.float32) for r in R], axis=0)
    y_prompt = cat("yp")
    y_sample = cat("ys")
    outs = (y_prompt, y_sample,
            cat("o_pC")[None], cat("o_pn")[None], cat("o_pm")[None], cat("o_pS")[None], cat("o_pcv")[None],
            cat("o_pmk").reshape(1, NCORES * NPS, NMEM, 4, 256), cat("o_pmv").reshape(1, NCORES * NPS, NMEM, 4, 256),
            cat("o_sC")[None], cat("o_sn")[None], cat("o_sm")[None], cat("o_sS")[None], cat("o_scv")[None])
    return outs
```

```python
import os
import numpy as np
from contextlib import ExitStack
import concourse.bass as bass
import concourse.mybir as mybir
from concourse.bass_utils import run_bass_kernel_spmd

F32 = mybir.dt.float32
BF16 = mybir.dt.bfloat16
F32R = mybir.dt.float32r
ALU = mybir.AluOpType
AF = mybir.ActivationFunctionType
AX = mybir.AxisListType

NCORES = 8
D = 1024
SEQ = 4096
NPS = 2
NSS = 4
LS = 16
NMEM = 256
EPS = 1e-6
NSLOT = 3
ND = 24
NF32 = int(os.environ.get("KNF32", "99"))


class Dep:
    __slots__ = ("w", "r", "excl", "wreal")

    def __init__(self, excl=False):
        self.w = None
        self.r = {}
        self.excl = excl
        self.wreal = True


class Eng:
    def __init__(self, name, h, sem):
        self.name = name
        self.h = h
        self.sem = sem
        self.cnt = 0
        self.seen = {}


class KB:
    def __init__(self, stage):
        self.stage = stage
        self.nc = bass.Bass("TRN2", target_bir_lowering=False)
        self.es = ExitStack()
        self.sem_owner = {}
        self.nwait = 0
        self.nins = 0

    def sb(self, name, shape, dt=F32):
        return self.es.enter_context(self.nc.sbuf_tensor(name, shape, dt))

    def psum(self, name, shape, dt=F32):
        return self.es.enter_context(self.nc.psum_tensor(name, shape, dt))

    def sem(self, name):
        return self.es.enter_context(self.nc.semaphore(name))

    def din(self, name, shape, dt=F32):
        return self.nc.dram_tensor(name, list(shape), dt, kind="ExternalInput").ap()

    def dout(self, name, shape, dt=F32):
        return self.nc.dram_tensor(name, list(shape), dt, kind="ExternalOutput").ap()

    def dint(self, name, shape, dt=F32):
        return self.nc.dram_tensor(name, list(shape), dt, kind="Internal").ap()

    def setup_engines(self):
        nc = self.nc
        self.PE = Eng("pe", nc.tensor, self.sem("s_pe"))
        self.ACT = Eng("act", nc.scalar, self.sem("s_act"))
        self.DVE = Eng("dve", nc.vector, self.sem("s_dve"))
        self.POOL = Eng("pool", nc.gpsimd, self.sem("s_pool"))
        self.SP = Eng("sp", nc.sync, self.sem("s_sp"))
        self.engs = [self.PE, self.ACT, self.DVE, self.POOL]
        for e in self.engs + [self.SP]:
            self.sem_owner[id(e.sem)] = e
        self.dsem = [self.sem("s_d%d" % i) for i in range(ND)]
        self.dcnt = [0] * ND
        self.dma_i = 0
        self.wsem = [self.sem("s_w%d" % i) for i in range(16)]
        self.wcnt = [0] * 16
        self.wma_i = 0

    def wait(self, E, tk, raw=False, pend=None):
        sem, val = tk
        if sem is E.sem:
            if E is self.PE or not raw:
                return
        own = self.sem_owner.get(id(sem))
        if own is not None:
            assert val <= own.cnt, "waiting on a ticket not yet emitted (%s on %s %d>%d)" % (E.name, own.name, val, own.cnt)
        if E.seen.get(id(sem), 0) >= val:
            return
        E.seen[id(sem)] = val
        self.nwait += 1
        if pend is None:
            E.h.wait_ge(sem, val)
            return
        for i_, (s_, v_) in enumerate(pend):
            if s_ is sem:
                pend[i_] = (sem, max(v_, val))
                return
        pend.append((sem, val))

    def _flush(self, E, pend):
        for s_, v_ in pend[:-1]:
            E.h.wait_ge(s_, v_)
        return pend[-1] if pend else None

    def _pre(self, E, reads, writes):
        pend = []
        for d in reads:
            if d.w is not None:
                self.wait(E, d.w, raw=(d.wreal if d.excl else True), pend=pend)
        for d in writes:
            if d.w is not None:
                self.wait(E, d.w, raw=True, pend=pend)
            for tk in d.r.values():
                self.wait(E, tk, pend=pend)
        return pend

    def _post(self, tk, reads, writes):
        for d in reads:
            if d.excl:
                d.w = tk
                d.wreal = False
                d.r = {}
                continue
            old = d.r.get(id(tk[0]))
            if old is None or old[1] < tk[1]:
                d.r[id(tk[0])] = tk
        for d in writes:
            d.w = tk
            d.wreal = True
            d.r = {}

    def op(self, E, fn, reads=(), writes=(), inc=True):
        fuse = self._flush(E, self._pre(E, reads, writes))
        ins = fn()
        if fuse is not None:
            ins._wait_ge(fuse[0], fuse[1])
        self.nins += 1
        if inc:
            E.cnt += 1
            ins.then_inc(E.sem, 1)
            tk = (E.sem, E.cnt)
        else:
            tk = (E.sem, E.cnt + 1)
        self._post(tk, reads, writes)
        return tk

    def dma(self, Q, out, in_, reads=(), writes=(), **kw):
        pend = self._pre(Q, reads, writes)
        if Q is self.POOL:
            sems, cnts = self.wsem, self.wcnt
            slot = self.wma_i % 16
            self.wma_i += 1
        else:
            sems, cnts = self.dsem, self.dcnt
            slot = self.dma_i % ND
            self.dma_i += 1
        sem = sems[slot]
        if cnts[slot] > 0:
            self.wait(Q, (sem, cnts[slot]), pend=pend)
        fuse = self._flush(Q, pend)
        cnts[slot] += 16
        ins = Q.h.dma_start(out=out, in_=in_, **kw)
        if fuse is not None:
            ins._wait_ge(fuse[0], fuse[1])
        ins.then_inc(sem, 16)
        tk = (sem, cnts[slot])
        self._post(tk, reads, writes)
        return tk

    def barrier(self):
        for E in self.engs:
            for Fe in self.engs:
                if Fe is not E and Fe.cnt > 0:
                    self.wait(E, (Fe.sem, Fe.cnt))

    def mm(self, out, lhsT, rhs, start, stop, reads, writes, inc=None):
        if inc is None:
            inc = stop
        nc = self.nc
        return self.op(self.PE, lambda: nc.tensor.matmul(out, lhsT=lhsT, rhs=rhs, start=start, stop=stop),
                       reads, writes, inc=inc)

    def tr(self, out, in_, ident, reads, writes, inc=True):
        nc = self.nc
        return self.op(self.PE, lambda: nc.tensor.transpose(out, in_, ident), reads, writes, inc=inc)

    def act(self, out, in_, func, reads, writes, bias=None, scale=None, accum=None):
        nc = self.nc
        kw = {}
        if bias is not None:
            kw["bias"] = bias
        if scale is not None:
            kw["scale"] = scale
        if accum is not None:
            kw["accum_out"] = accum
        return self.op(self.ACT, lambda: nc.scalar.activation(out=out, in_=in_, func=func, **kw), reads, writes)

    def vec(self, E, name, reads, writes, **kw):
        h = E.h
        return self.op(E, lambda: getattr(h, name)(**kw), reads, writes)


def build(stage):
    k = KB(stage)
    nc = k.nc
    xp = k.din("xp", [NPS, SEQ, D])
    xs = k.din("xs", [NSS, LS, D])
    sC = k.din("sC", [NSS, 4, 128, 128])
    sn = k.din("sn", [NSS, 4, 128])
    sm = k.din("sm", [NSS, 4])
    sS = k.din("sS", [NSS, 4, 128, 128])
    scv = k.din("scv", [NSS, 3, 1536])
    ck = k.din("ck", [NSS, NMEM, D])
    cv = k.din("cv", [NSS, NMEM, D])
    memp = k.din("memp", [NPS, NMEM, D])
    w_in = k.din("w_in", [D, 4112])
    w_out = k.din("w_out", [D, D])
    wq = k.din("wq", [D, D])
    wk = k.din("wk", [D, D])
    wv = k.din("wv", [D, D])
    wo = k.din("wo", [D, D])
    w1 = k.din("w1", [D, 4096])
    w2 = k.din("w2", [4096, D])
    nw_mix = k.din("nw_mix", [D])
    nw_x = k.din("nw_x", [D])
    nw_mem = k.din("nw_mem", [D])
    nw_ffn = k.din("nw_ffn", [D])
    nw_fin = k.din("nw_fin", [D])
    b_ig = k.din("b_ig", [4])
    b_fg = k.din("b_fg", [4])
    nw_a = k.din("nw_a", [128])
    nw_b = k.din("nw_b", [128])
    convw = k.din("convw", [4, 1536])
    a_log = k.din("a_log", [4])
    dt_b = k.din("dt_b", [4])

    yp = k.dout("yp", [NPS, SEQ, D])
    ys = k.dout("ys", [NSS, LS, D])
    o_pC = k.dout("o_pC", [NPS, 4, 128, 128])
    o_pn = k.dout("o_pn", [NPS, 4, 128])
    o_pm = k.dout("o_pm", [NPS, 4])
    o_pS = k.dout("o_pS", [NPS, 4, 128, 128])
    o_pcv = k.dout("o_pcv", [NPS, 3, 1536])
    o_pmk = k.dout("o_pmk", [NPS, NMEM, D])
    o_pmv = k.dout("o_pmv", [NPS, NMEM, D])
    o_sC = k.dout("o_sC", [NSS, 4, 128, 128])
    o_sn = k.dout("o_sn", [NSS, 4, 128])
    o_sm = k.dout("o_sm", [NSS, 4])
    o_sS = k.dout("o_sS", [NSS, 4, 128, 128])
    o_scv = k.dout("o_scv", [NSS, 3, 1536])

    scr = {}
    for nm, nb in (("in", 8), ("out", 2), ("q", 2), ("k", 2), ("v", 2), ("o", 2), ("1", 8), ("2", 8)):
        scr[nm] = k.dint("scr_" + nm, [nb, 128, 4096], BF16)

    k.setup_engines()
    PE, ACT, DVE, POOL, SP = k.PE, k.ACT, k.DVE, k.POOL, k.SP

    X = k.sb("X", [128, 4, D])
    XN = k.sb("XN", [128, 2, D], BF16)
    XNT = k.sb("XNT", [128, 8, 512], BF16)
    WR = k.sb("WR", [128, NSLOT, 8, 512], BF16)
    GW = k.sb("GW", [128, 8, 16], BF16)
    ARA = k.sb("ARA", [128, 8192])
    ARM = k.sb("ARM", [128, 8192])
    QKA = ARA[:, 0:4096].rearrange("p (c n) -> p c n", c=8)
    KA = ARA[:, 4096:6144].rearrange("p (c n) -> p c n", c=4)
    OG = ARA[:, 6144:7168].bitcast(BF16).rearrange("p (c n) -> p c n", c=4)
    ZG = ARA[:, 7168:8192].bitcast(BF16).rearrange("p (c n) -> p c n", c=4)
    POST = ARM[:, 0:6144].rearrange("p (c n) -> p c n", c=12)
    POSTF = POST
    PRE = ARM[:, 6144:6144 + 3 * 515].rearrange("p (c n) -> p c n", c=3)
    QF = ARA[:, 0:2048].bitcast(BF16).rearrange("p (c n) -> p c n", c=8)
    PTS = ARA[:, 2048:4096].bitcast(BF16).rearrange("p (c n) -> p c n", c=8)
    OTB = ARA[:, 4096:6144].bitcast(BF16).rearrange("p (c n) -> p c n", c=8)
    PF = ARA[:, 6144:7168]
    PN = ARA[:, 7168:7680].bitcast(BF16)
    HID = ARM[:, :].bitcast(BF16).rearrange("p (c n) -> p c n", c=32)
    VA = k.sb("VA", [128, 4, 4, 130], F32R)
    GT = k.sb("GT", [128, 4, 16])
    HT = k.sb("HT", [128, 8, 512], BF16)
    H = k.sb("H", [128, 1, D], BF16)
    HIST4 = k.sb("HIST4", [128, 4, 12, 3])
    DG = k.sb("DG", [4, 4])
    TS = k.sb("TS", [48, 128])
    HS = k.sb("HS", [128, 36])
    dHS = Dep()
    dTS = Dep()
    BLG = k.sb("BLG", [128, 8])
    BLGb = k.sb("BLGb", [128, 8])
    BLG2 = [BLG, BLGb]
    KT = k.sb("KT", [128, 8, NMEM], BF16)
    VV = k.sb("VV", [128, 2, D], BF16)
    CA = k.sb("CA", [128, 2, 4, 130], F32R)
    SS = k.sb("SS", [128, 2, 4, 128], F32R)
    ident = k.sb("ident", [128, 128])
    identb = k.sb("identb", [128, 128], BF16)
    tri = k.sb("tri", [128, 128])
    maskA = k.sb("maskA", [128, 128])
    mstrict = k.sb("mstrict", [128, 128])
    ones = k.sb("ones", [128, 128])
    WBF = k.sb("WBF", [128, D])
    WN = k.sb("WN", [128, 4, 8])
    NWA = k.sb("NWA", [128, 128])
    NWB = k.sb("NWB", [128, 128])
    GB = k.sb("GB", [128, 16])
    SGN = k.sb("SGN", [128, 16])
    NEG8 = k.sb("NEG8", [128, 8])
    WCV = k.sb("WCV", [128, 12, 4])
    ST = k.sb("ST", [128, 64])
    SM = k.sb("SM", [128, 256])
    JUNK = k.sb("JUNK", [128, D], BF16)
    TT = [k.sb("T%d" % i, [128, 512]) for i in range(5)]
    dTT = [Dep() for _ in range(5)]
    QSR, SMKR, QGR, QKMR = [k.sb(n_, [128, 512], F32R) for n_ in ("QSR", "SMKR", "QGR", "QKMR")]
    dQSR, dSMKR, dQGR, dQKMR = Dep(), Dep(), Dep(), Dep()
    RB = k.sb("RB", [128, 8, 128])
    NY = k.sb("NY", [128, 1, 4, 128 if NF32 < 7 else 2], BF16)
    NZ = k.sb("NZ", [128, 2, 4, 128 if NF32 < 7 else 2], BF16)
    NP_ = k.sb("NP", [128, 1, 4, 128], F32R)
    NPB = k.sb("NPB", [128, 1, 4, 128 if NF32 < 7 else 2], BF16)
    dNPB = Dep()
    NYF = k.sb("NYF", [128, 4, 128], F32R)
    NZF = k.sb("NZF", [128, 2, 4, 128], F32R)
    dNYF = Dep()
    dNZF = [Dep(), Dep()]
    dNYFh = [Dep(), Dep()]
    dNZFh = [[Dep(), Dep()], [Dep(), Dep()]]
    dNPh = [Dep(), Dep()]
    GK = k.sb("GK", [128, 4, 128], F32R)
    KD = k.sb("KD", [128, 4, 128], F32R)
    VB = k.sb("VB", [128, 4, 128], F32R)
    BU = k.sb("BU", [128, 4, 128], F32R)
    WKT = k.sb("WKT", [128, 4, 128], F32R)
    KW = k.sb("KW", [128, 4, 128], F32R)
    M4 = k.sb("M4", [4, 8])
    BBC = k.sb("BBC", [128, 4])
    M0B = k.sb("M0B", [128, 4])
    STG = k.sb("STG", [128, 4, 129])
    AG = k.sb("AG", [128, 8])

    PT = [k.psum("PT%d" % i, [128, 1024], BF16) for i in range(2)]
    PS = [k.psum("PS%d" % i, [128, 512]) for i in range(6)]
    ptd = [Dep(True), Dep(True)]
    psd = [Dep(True) for _ in range(6)]
    rr = {"pt": 0, "ps": 0, "tt": 0, "pre": 0}

    def pt():
        i = rr["pt"] % 2
        rr["pt"] += 1
        return PT[i], ptd[i]

    def ps():
        i = rr["ps"] % 6
        rr["ps"] += 1
        return PS[i], psd[i]

    def psA():
        i = rr.setdefault("psa", 0) % 3
        rr["psa"] += 1
        return PS[i], psd[i]

    def psB():
        i = 3 + rr.setdefault("psb", 0) % 3
        rr["psb"] += 1
        return PS[i], psd[i]

    def tmp():
        i = rr["tt"] % 5
        rr["tt"] += 1
        return TT[i], dTT[i]

    dX = [Dep() for _ in range(4)]
    dST = [Dep(), Dep()]
    dSTF = [Dep(), Dep()]
    dXN = [Dep(), Dep()]
    dXNT = Dep()
    dWR = [Dep() for _ in range(NSLOT)]
    dconst = Dep()

    def pool(name, reads=(), writes=(), **kw):
        return k.vec(POOL, name, reads, writes, **kw)

    def dve(name, reads=(), writes=(), **kw):
        return k.vec(DVE, name, reads, writes, **kw)

    pool("memset", writes=[dconst], ap=ones[:], constant=1.0)
    pool("memset", writes=[dconst], ap=ident[:], constant=1.0)
    k.op(POOL, lambda: nc.gpsimd.affine_select(out=ident[:], in_=ident[:], pattern=[[-1, 128]], compare_op=ALU.is_equal,
                                                fill=0.0, base=0, channel_multiplier=1), [dconst], [dconst])
    pool("tensor_copy", [dconst], [dconst], out=identb[:], in_=ident[:])
    pool("memset", writes=[dconst], ap=tri[:], constant=1.0)
    k.op(POOL, lambda: nc.gpsimd.affine_select(out=tri[:], in_=tri[:], pattern=[[1, 128]], compare_op=ALU.is_ge,
                                                fill=0.0, base=0, channel_multiplier=-1), [dconst], [dconst])
    pool("memset", writes=[dconst], ap=mstrict[:], constant=1.0)
    k.op(POOL, lambda: nc.gpsimd.affine_select(out=mstrict[:], in_=mstrict[:], pattern=[[1, 128]], compare_op=ALU.is_gt,
                                                fill=0.0, base=0, channel_multiplier=-1), [dconst], [dconst])
    pool("tensor_scalar_mul", [dconst], [dconst], out=maskA[:], in0=tri[:], scalar1=128.0 ** -0.5)
    pool("memset", writes=[dconst], ap=SGN[:], constant=1.0)
    pool("memset", writes=[dconst], ap=SGN[:, 4:8], constant=-1.0)
    pool("memset", writes=[dconst], ap=SGN[:, 12:16], constant=-1.0)
    pool("memset", writes=[dconst], ap=GB[:], constant=0.0)
    pool("memset", writes=[dconst], ap=NEG8[:], constant=-1.0)

    k.dma(SP, WBF[:], nw_fin.partition_broadcast(128), writes=[dconst])
    def load_T(dst_view, src2d, rows, wdeps=()):
        k.dma(SP, TS[:rows, :], src2d, writes=[dTS])
        p, dp = ps()
        k.tr(p[:, :rows], TS[:rows, :], ident[:rows, :rows], [dTS, dconst], [dp])
        dve("tensor_copy", [dp], [dconst] + list(wdeps), out=dst_view, in_=p[:, :rows])

    def store_T(dst2d, src_view, rows, rdeps):
        p, dp = ps()
        k.tr(p[:rows, :128], src_view, ident[:, :], list(rdeps) + [dconst], [dp])
        k.act(TS[:rows, :], p[:rows, :128], AF.Copy, [dp], [dTS])
        k.dma(SP, dst2d, TS[:rows, :], reads=[dTS])

    k.dma(SP, NWA[:], nw_a.partition_broadcast(128), writes=[dconst])
    k.dma(SP, NWB[:], nw_b.partition_broadcast(128), writes=[dconst])
    k.dma(SP, GB[:, 0:4], b_ig.partition_broadcast(128), writes=[dconst])
    k.dma(SP, GB[:, 4:8], b_fg.partition_broadcast(128), writes=[dconst])
    k.dma(SP, GB[:, 8:12], dt_b.partition_broadcast(128), writes=[dconst])
    k.dma(SP, ST[:, 0:4], a_log.partition_broadcast(128), writes=[dconst])
    for i, src in enumerate((nw_mix, nw_x, nw_ffn, nw_mem)):
        load_T(WN[:, i, :], src.rearrange("(k p) -> k p", p=128), 8)
    load_T(WCV[:, :, :].rearrange("p c j -> p j c"), convw.rearrange("j (c p) -> (j c) p", p=128), 48)
    k.act(ST[:, 4:8], ST[:, 0:4], AF.Exp, [dconst], [dconst])
    dve("tensor_scalar_mul", [dconst], [dconst], out=NEG8[:, 4:8], in0=ST[:, 4:8], scalar1=-1.0)
    ones16 = ones[:, 0:16].rearrange("p (a b c) -> p a b c", a=4, b=4)
    dve("tensor_copy", [dconst], [dconst], out=VA[:, :, :, 128:129], in_=ones16)
    dve("tensor_scalar_mul", [dconst], [dconst], out=VA[:, :, :, 129:130], in0=ones16, scalar1=0.0)

    dscr = {}

    def cast_blk(nm, b_, src_rows_cols):
        d_ = dscr.setdefault((nm, b_), Dep())
        k.dma(POOL, scr[nm][b_].rearrange("p (k c) -> p k c", k=8), src_rows_cols.rearrange("(k p) c -> p k c", p=128),
              writes=[d_])

    in_cols = [0, 512, 1024, 1536, 2056, 2568, 3080, 3592]
    if stage < 2:
        for b in range(2):
            cast_blk("k", b, wk[:, b * 512:(b + 1) * 512])
        for b in range(2):
            cast_blk("v", b, wv[:, b * 512:(b + 1) * 512])
    if stage >= 2:
        for b in (4, 0, 5, 1, 6, 2, 3, 7):
            cast_blk("in", b, w_in[:, in_cols[b]:in_cols[b] + 512])
        for kc in range(8):
            k.dma(POOL, GW[:, kc, 0:8], w_in[kc * 128:(kc + 1) * 128, 2048:2056], writes=[dconst])
            k.dma(POOL, GW[:, kc, 8:16], w_in[kc * 128:(kc + 1) * 128, 4104:4112], writes=[dconst])
        for nm, src in (("out", w_out), ("q", wq), ("o", wo)):
            for b in range(2):
                cast_blk(nm, b, src[:, b * 512:(b + 1) * 512])
        for b in range(8):
            cast_blk("1", b, w1[:, b * 512:(b + 1) * 512])
        for cb in range(2):
            for fb in range(4):
                cast_blk("2", cb * 4 + fb, w2[fb * 1024:(fb + 1) * 1024, cb * 512:(cb + 1) * 512])
        for b in range(2):
            cast_blk("k", b, wk[:, b * 512:(b + 1) * 512])
        for b in range(2):
            cast_blk("v", b, wv[:, b * 512:(b + 1) * 512])

    grp_seq = [("in", b) for b in (4, 0, 5, 1, 6, 2, 3, 7)] + [("out", 0), ("out", 1), ("q", 0), ("q", 1), ("o", 0), ("o", 1)] \
        + [("1", b) for b in range(8)] + [("2", b) for b in range(8)]
    mem_seq = [("k", 0), ("k", 1), ("v", 0), ("v", 1)]
    NGP = SEQ // 512
    KNG = int(os.environ.get("KNG", str(NGP)))
    if stage >= 2:
        wseq = list(grp_seq)
        for j in range(NPS):
            wseq += mem_seq
            for g in range(KNG):
                wseq += grp_seq
    else:
        wseq = mem_seq + mem_seq
    wst = {"issued": 0, "used": 0}

    def wblk(nm, b, ahead=NSLOT - 1):
        i = wst["used"]
        assert wseq[i] == (nm, b), (i, wseq[i], nm, b)
        while wst["issued"] < min(i + 1 + ahead, len(wseq)):
            ii = wst["issued"]
            nm2, b2 = wseq[ii]
            slot2 = ii % NSLOT
            k.dma(SP, WR[:, slot2].rearrange("p k c -> p (k c)"), scr[nm2][b2], reads=[dscr[(nm2, b2)]], writes=[dWR[slot2]])
            wst["issued"] += 1
        wst["used"] += 1
        slot = i % NSLOT
        return WR[:, slot], dWR[slot]

    def norm_pre(src_ap, dsrc, L, par, extra_reads=()):
        dst_ = dST[par]
        k.act(JUNK[:L, :], src_ap, AF.Square, [dsrc] + list(extra_reads), [dst_], accum=ST[:L, 8 + par:9 + par])
        k.act(ST[:L, 10 + par:11 + par], ST[:L, 8 + par:9 + par], AF.Ln, [dst_], [dst_], bias=EPS, scale=1.0 / D)
        k.act(ST[:L, 12 + par:13 + par], ST[:L, 10 + par:11 + par], AF.Exp, [dst_], [dst_], scale=-0.5)
        dve("tensor_scalar_mul", [dsrc, dst_] + list(extra_reads), [dXN[par]], out=XN[:L, par, :], in0=src_ap,
            scalar1=ST[:L, 12 + par:13 + par])

    def norm_tr(L, col, wn_idx, par, dst=None, ddst=None):
        if dst is None:
            dst, ddst = XNT, dXNT
        p, dp = pt()
        pv = p[:].rearrange("p (k c) -> p k c", k=8)
        for kc in range(8):
            k.tr(pv[:, kc, :L], XN[:L, par, kc * 128:(kc + 1) * 128], identb[:L, :L], [dXN[par], dconst], [dp],
                 inc=(kc == 7))
        dve("tensor_tensor", [dp, dconst], [ddst], out=dst[:, :, col:col + L], in0=pv[:, :, :L],
            in1=WN[:, wn_idx, :].unsqueeze(2).to_broadcast([128, 8, L]), op=ALU.mult)

    def rmsnorm_to_xnt(src_ap, dsrc, L, col, wn_idx, par, dst=None, ddst=None, extra_reads=()):
        norm_pre(src_ap, dsrc, L, par, extra_reads)
        norm_tr(L, col, wn_idx, par, dst, ddst)

    def load_x(i, src_ap, L):
        k.dma(SP, X[:L, i, :], src_ap, writes=[dX[i]])

    dKT = Dep()
    dVV = Dep()
    MT = HT
    dMT = Dep()

    def mem_kv(j):
        k.barrier()
        for t in range(2):
            load_x(t, memp[j, t * 128:(t + 1) * 128, :], 128)
            rmsnorm_to_xnt(X[:, t, :], dX[t], 128, t * 128, 3, t, dst=MT, ddst=dMT)
        for which, out_d in (("k", o_pmk), ("v", o_pmv)):
            for cb in range(2):
                w, dw = wblk(which, cb)
                for t in range(2):
                    p, dp = ps()
                    for kc in range(8):
                        k.mm(p[:, :], MT[:, kc, t * 128:(t + 1) * 128], w[:, kc, :], kc == 0, kc == 7, [dMT, dw], [dp])
                    tt, dtt = tmp()
                    k.act(tt[:, :], p[:, :], AF.Copy, [dp], [dtt])
                    k.dma(SP, out_d[j, t * 128:(t + 1) * 128, cb * 512:(cb + 1) * 512], tt[:, :], reads=[dtt])
                    if which == "v":
                        dve("tensor_copy", [dp], [dVV], out=VV[:, t, cb * 512:(cb + 1) * 512], in_=p[:, :])
                if which == "k":
                    for jj in range(4):
                        p, dp = ps()
                        for kc in range(8):
                            k.mm(p[:, :NMEM], w[:, kc, jj * 128:(jj + 1) * 128], MT[:, kc, :NMEM], kc == 0, kc == 7,
                                 [dMT, dw], [dp])
                        k.act(KT[:, cb * 4 + jj, :], p[:, :NMEM], AF.Copy, [dp], [dKT])
        k.barrier()

    def sample_kv(j):
        for t in range(2):
            for half in range(2):
                tk_, dtk = tmp()
                k.dma(SP, tk_[:, :], ck[j, t * 128:(t + 1) * 128, half * 512:(half + 1) * 512], writes=[dtk])
                dve("tensor_copy", [dtk], [dXN[0]], out=XN[:, 0, half * 512:(half + 1) * 512], in_=tk_[:, :])
                tv_, dtv = tmp()
                k.dma(SP, tv_[:, :], cv[j, t * 128:(t + 1) * 128, half * 512:(half + 1) * 512], writes=[dtv])
                dve("tensor_copy", [dtv], [dVV], out=VV[:, t, half * 512:(half + 1) * 512], in_=tv_[:, :])
            p, dp = pt()
            pv = p[:].rearrange("p (k c) -> p k c", k=8)
            for kc in range(8):
                k.tr(pv[:, kc, :], XN[:, 0, kc * 128:(kc + 1) * 128], identb[:, :], [dXN[0], dconst], [dp],
                     inc=(kc == 7))
            k.act(KT[:, :, t * 128:(t + 1) * 128], pv[:, :, :], AF.Copy, [dp], [dKT])

    dQKA = [Dep() for _ in range(8)]
    dKA = [Dep() for _ in range(4)]
    dVA = [Dep() for _ in range(4)]
    dOG = [Dep() for _ in range(4)]
    dZG = [Dep() for _ in range(4)]
    dGT = [Dep() for _ in range(4)]
    dPOST = [Dep() for _ in range(12)]
    dPRE = [Dep() for _ in range(3)]
    dHIST = [[Dep() for _ in range(12)] for _ in range(4)]
    dH = [Dep(), Dep()]
    dHT = Dep()
    dCA = [Dep(), Dep()]
    dSS = [Dep(), Dep()]
    dSMALL = Dep()
    dSg, dSa, dSb = Dep(), Dep(), Dep()
    dSgp = [Dep(), Dep()]
    dBLGp = [Dep(), Dep()]
    dRBg = Dep()
    dM4 = Dep()
    dBBC = Dep()
    state = {"cur": 0}

    class Tile:
        def __init__(self, L, col, src, dst, seg):
            self.L, self.col, self.src, self.dst, self.seg = L, col, src, dst, seg

    class Seg:
        def __init__(self, kind, j, col0, n, first, last, sidx=0):
            self.kind, self.j, self.col0, self.n, self.first, self.last, self.sidx = kind, j, col0, n, first, last, sidx

    def proj_tm(w, dw, t, ncols=512):
        p, dp = ps()
        for kc in range(8):
            k.mm(p[:t.L, :ncols], XNT[:, kc, t.col:t.col + t.L], w[:, kc, :ncols], kc == 0, kc == 7, [dXNT, dw], [dp])
        return p, dp

    def proj_fm(w, dw, j, NT, src=None, dsrc=None):
        if src is None:
            src, dsrc = XNT, dXNT
        p, dp = ps()
        for kc in range(8):
            k.mm(p[:, :NT], w[:, kc, j * 128:(j + 1) * 128], src[:, kc, :NT], kc == 0, kc == 7, [dsrc, dw], [dp])
        return p, dp

    def interleave(gens, K):
        active = []
        it = iter(gens)
        while True:
            while len(active) < K:
                g = next(it, None)
                if g is None:
                    break
                active.append(g)
            if not active:
                break
            for g in list(active):
                try:
                    next(g)
                except StopIteration:
                    active.remove(g)

    def sigmoid_gen(w, dw, t, out_view, dout, nw, mul_in=False):
        L = t.L
        p, dp = proj_tm(w, dw, t)
        tt, dtt = tmp()
        k.act(tt[:L, :], p[:L, :], AF.Exp, [dp], [dtt], scale=-1.0)
        yield
        k.act(tt[:L, :], tt[:L, :], AF.Ln, [dtt], [dtt], bias=1.0)
        yield
        k.act(tt[:L, :], tt[:L, :], AF.Exp, [dtt], [dtt], scale=-1.0)
        yield
        if mul_in:
            dve("tensor_tensor", [dtt, dp], [dtt], out=tt[:L, :], in0=p[:L, :], in1=tt[:L, :], op=ALU.mult)
        yield
        pool("tensor_tensor", [dtt, dconst], [dout], out=out_view,
             in0=tt[:L, :].rearrange("p (h d) -> p h d", h=4),
             in1=nw[:L, :].unsqueeze(1).to_broadcast([L, 4, 128]), op=ALU.mult)

    def conv_chunk(c, p, dp, seg):
        for _ in conv_gen(c, p, dp, seg):
            pass

    def conv_gen(c, p, dp, seg):
        n, c0 = seg.n, seg.col0
        HIST = HIST4[:, seg.sidx]
        dH_ = dHIST[seg.sidx]
        r = rr["pre"] % 3
        rr["pre"] += 1
        pre = PRE[:, r, :]
        k.act(pre[:, 3:3 + n], p[:, c0:c0 + n], AF.Copy, [dp], [dPRE[r]])
        pool("tensor_copy", [dH_[c]], [dPRE[r]], out=pre[:, 0:3], in_=HIST[:, c, :])
        pool("tensor_copy", [dPRE[r]], [dH_[c]], out=HIST[:, c, :], in_=pre[:, n:n + 3])
        po = POST[:, c, c0:c0 + n]
        yield
        dve("tensor_scalar_mul", [dPRE[r], dconst], [dPOST[c]], out=po, in0=pre[:, 0:n], scalar1=WCV[:, c, 0:1])
        for j in range(1, 4):
            dve("scalar_tensor_tensor", [dPRE[r], dconst, dPOST[c]], [dPOST[c]], out=po, in0=pre[:, j:j + n],
                 scalar=WCV[:, c, j:j + 1], in1=po, op0=ALU.mult, op1=ALU.add)
        yield
        tt, dtt = tmp()
        k.act(tt[:, :n], po, AF.Exp, [dPOST[c]], [dtt], scale=-1.0)
        yield
        k.act(tt[:, :n], tt[:, :n], AF.Ln, [dtt], [dtt], bias=1.0)
        yield
        k.act(tt[:, :n], tt[:, :n], AF.Exp, [dtt], [dtt], scale=-1.0)
        yield
        pool("tensor_tensor", [dtt, dPOST[c]], [dPOST[c]], out=po, in0=po, in1=tt[:, :n], op=ALU.mult)
        if c < 8:
            yield
            t2, dt2 = tmp()
            k.act(t2[:, :n], po, AF.Square, [dPOST[c]], [dt2])
            p2, dp2 = ps()
            sc = 128.0 if c < 4 else 1.0
            k.mm(p2[:, :n], ones[:, :], t2[:, :n], True, True, [dt2, dconst], [dp2])
            yield
            k.act(t2[:, :n], p2[:, :n], AF.Ln, [dp2], [dt2], bias=EPS * sc, scale=sc)
            k.act(t2[:, :n], t2[:, :n], AF.Exp, [dt2], [dt2], scale=-0.5)
            yield
            pool("tensor_tensor", [dt2, dPOST[c]], [dPOST[c]], out=po, in0=po, in1=t2[:, :n], op=ALU.mult)

    free_ps = list(range(6))
    free_tt = list(range(5))
    free_pre = list(range(3))

    XP = ARA[:, 4096:8192].rearrange("p (c n) -> p c n", c=4)
    dXP = [Dep() for _ in range(4)]

    def prefetch_x(next_tiles):
        for E in k.engs:
            if E.cnt > 0:
                k.wait(SP, (E.sem, E.cnt))
        for i, t in enumerate(next_tiles):
            k.dma(SP, XP[:t.L, i, :], t.src, writes=[dXP[i]])

    def phase_in(tiles, segs, NT, pref=False):
        for i, t in enumerate(tiles):
            if pref:
                if i < 2:
                    pool("tensor_copy", [dXP[i]] + dKA + dOG + dZG, [dX[i]], out=X[:t.L, i, :], in_=XP[:t.L, i, :])
                else:
                    k.act(X[:t.L, i, :], XP[:t.L, i, :], AF.Copy, [dXP[i]] + dKA + dOG + dZG, [dX[i]])
            else:
                load_x(i, t.src, t.L)
                rmsnorm_to_xnt(X[:t.L, i, :], dX[i], t.L, t.col, 0, i % 2)
        single = (len(segs) == 1)
        wcache = {}

        def getw(blk):
            if blk not in wcache:
                wcache[blk] = wblk("in", blk, ahead=1)
            return wcache[blk]

        def acq(lst):
            while not lst:
                yield
            return lst.pop(0)

        def fm_gen(blk, h, dst_chunk):
            w, dw = getw(blk)
            bi = yield from acq(free_ps)
            p, dp = PS[bi], psd[bi]
            for kc in range(8):
                k.mm(p[:, :NT], w[:, kc, h * 128:(h + 1) * 128], XNT[:, kc, :NT], kc == 0, kc == 7, [dXNT, dw], [dp])
            k.act(QKA[:, dst_chunk, :NT], p[:, :NT], AF.Copy, [dp], [dQKA[dst_chunk]])
            free_ps.append(bi)
            yield

        def tm_gen(blk, i, t, kind):
            L = t.L
            bi = yield from acq(free_ps)
            p, dp = PS[bi], psd[bi]
            if kind == "gates":
                for kc in range(8):
                    k.mm(p[:L, :16], XNT[:, kc, t.col:t.col + L], GW[:, kc, :], kc == 0, kc == 7, [dXNT, dconst], [dp])
                k.act(GT[:L, i, :], p[:L, :16], AF.Copy, [dp], [dGT[i]])
                free_ps.append(bi)
                yield
                return
            w, dw = getw(blk)
            for kc in range(8):
                k.mm(p[:L, :], XNT[:, kc, t.col:t.col + L], w[:, kc, :], kc == 0, kc == 7, [dXNT, dw], [dp])
            if kind == "ka":
                k.act(KA[:L, i, :], p[:L, :], AF.Copy, [dp], [dKA[i]])
                free_ps.append(bi)
                yield
                return
            if kind == "va":
                k.act(VA[:L, i, :, 0:128], p[:L, :].rearrange("p (h d) -> p h d", h=4), AF.Copy, [dp], [dVA[i]])
                free_ps.append(bi)
                yield
                return
            ti = yield from acq(free_tt)
            tt, dtt = TT[ti], dTT[ti]
            k.act(tt[:L, :], p[:L, :], AF.Exp, [dp], [dtt], scale=-1.0)
            if kind == "oa":
                free_ps.append(bi)
            yield
            k.act(tt[:L, :], tt[:L, :], AF.Ln, [dtt], [dtt], bias=1.0)
            yield
            k.act(tt[:L, :], tt[:L, :], AF.Exp, [dtt], [dtt], scale=-1.0)
            yield
            if kind == "zb":
                dve("tensor_tensor", [dtt, dp], [dtt], out=tt[:L, :], in0=p[:L, :], in1=tt[:L, :], op=ALU.mult)
                free_ps.append(bi)
                yield
            outv, dout, nw = (OG, dOG, NWA) if kind == "oa" else (ZG, dZG, NWB)
            pool("tensor_tensor", [dtt, dconst], [dout[i]], out=outv[:L, i, :].rearrange("p (h d) -> p h d", h=4),
                 in0=tt[:L, :].rearrange("p (h d) -> p h d", h=4),
                 in1=nw[:L, :].unsqueeze(1).to_broadcast([L, 4, 128]), op=ALU.mult)
            free_tt.append(ti)
            yield

        def conv_item(c):
            w, dw = getw(4 + c // 4)
            bi = yield from acq(free_ps)
            p, dp = PS[bi], psd[bi]
            h = c % 4
            for kc in range(8):
                k.mm(p[:, :NT], w[:, kc, h * 128:(h + 1) * 128], XNT[:, kc, :NT], kc == 0, kc == 7, [dXNT, dw], [dp])
            for seg in segs:
                n, c0 = seg.n, seg.col0
                HIST = HIST4[:, seg.sidx]
                dH_ = dHIST[seg.sidx]
                r = yield from acq(free_pre)
                pre = PRE[:, r, :]
                k.act(pre[:, 3:3 + n], p[:, c0:c0 + n], AF.Copy, [dp], [dPRE[r]])
                if seg is segs[-1]:
                    free_ps.append(bi)
                pool("tensor_copy", [dH_[c]], [dPRE[r]], out=pre[:, 0:3], in_=HIST[:, c, :])
                pool("tensor_copy", [dPRE[r]], [dH_[c]], out=HIST[:, c, :], in_=pre[:, n:n + 3])
                po = POST[:, c, c0:c0 + n]
                yield
                dve("tensor_scalar_mul", [dPRE[r], dconst], [dPOST[c]], out=po, in0=pre[:, 0:n], scalar1=WCV[:, c, 0:1])
                for j in range(1, 4):
                    dve("scalar_tensor_tensor", [dPRE[r], dconst, dPOST[c]], [dPOST[c]], out=po, in0=pre[:, j:j + n],
                        scalar=WCV[:, c, j:j + 1], in1=po, op0=ALU.mult, op1=ALU.add)
                free_pre.append(r)
                yield
                ti = yield from acq(free_tt)
                tt, dtt = TT[ti], dTT[ti]
                k.act(tt[:, :n], po, AF.Exp, [dPOST[c]], [dtt], scale=-1.0)
                yield
                k.act(tt[:, :n], tt[:, :n], AF.Ln, [dtt], [dtt], bias=1.0)
                yield
                k.act(tt[:, :n], tt[:, :n], AF.Exp, [dtt], [dtt], scale=-1.0)
                yield
                pool("tensor_tensor", [dtt, dPOST[c]], [dPOST[c]], out=po, in0=po, in1=tt[:, :n], op=ALU.mult)
                if c < 8:
                    yield
                    k.act(tt[:, :n], po, AF.Square, [dPOST[c]], [dtt])
                    b2 = yield from acq(free_ps)
                    p2, dp2 = PS[b2], psd[b2]
                    sc = 128.0 if c < 4 else 1.0
                    k.mm(p2[:, :n], ones[:, :], tt[:, :n], True, True, [dtt, dconst], [dp2])
                    yield
                    k.act(tt[:, :n], p2[:, :n], AF.Ln, [dp2], [dtt], bias=EPS * sc, scale=sc)
                    free_ps.append(b2)
                    yield
                    k.act(tt[:, :n], tt[:, :n], AF.Exp, [dtt], [dtt], scale=-0.5)
                    yield
                    pool("tensor_tensor", [dtt, dPOST[c]], [dPOST[c]], out=po, in0=po, in1=tt[:, :n], op=ALU.mult)
                free_tt.append(ti)
                yield

        def alt(a, b_):
            out = []
            for i_ in range(max(len(a), len(b_))):
                if i_ < len(a):
                    out.append(a[i_])
                if i_ < len(b_):
                    out.append(b_[i_])
            return out

        T_ = list(enumerate(tiles))
        items = []
        items += alt([conv_item(c) for c in range(0, 4)], [fm_gen(0, h, h) for h in range(4)])
        items += alt([conv_item(c) for c in range(4, 8)],
                     [fm_gen(1, h, 4 + h) for h in range(4)] + [tm_gen(1, i, t, "ka") for i, t in T_])
        items += alt([conv_item(c) for c in range(8, 12)], [tm_gen(2, i, t, "va") for i, t in T_])
        items += alt([tm_gen(3, i, t, "oa") for i, t in T_], [tm_gen(None, i, t, "gates") for i, t in T_])
        items += [tm_gen(7, i, t, "zb") for i, t in T_]
        interleave(items, 6 if single else 2)
        assert len(free_ps) == 6 and len(free_tt) == 5 and len(free_pre) == 3

    def smc(a, b, L):
        return SM[:L, a:b]

    def hist_init(seg):
        HIST = HIST4[:, seg.sidx]
        dH_ = dHIST[seg.sidx]
        if seg.kind == "p":
            for c in range(12):
                pool("memset", writes=[dH_[c]], ap=HIST[:, c, :], constant=0.0)
        else:
            load_T(HIST[:, :, :].rearrange("p c t -> p t c"), scv[seg.j].rearrange("t (c p) -> (t c) p", p=128), 36, dH_)

    def state_init(seg):
        cur = state["cur"]
        if seg.kind == "p":
            dve("tensor_scalar_mul", [dconst], [dCA[cur]], out=CA[:, cur], scalar1=0.0,
                in0=ones[:, 0:4].unsqueeze(2).to_broadcast([128, 4, 130]))
            dve("tensor_scalar_mul", [dconst], [dSS[cur]], out=SS[:, cur], scalar1=0.0,
                in0=ones[:, 0:4].unsqueeze(2).to_broadcast([128, 4, 128]))
            pool("memset", writes=[dM4], ap=M4[:, :], constant=0.0)
            pool("memset", writes=[dBBC], ap=BBC[:, :], constant=0.0)
        else:
            j = seg.j
            k.dma(SP, STG[:, :, 0:128], sC[j].rearrange("h a b -> a h b"), writes=[dSTG])
            load_T(STG[:, :, 128], sn[j], 4, [dSTG])
            tss, dtss = tmp()
            k.dma(SP, tss[:, :].rearrange("p (h d) -> p h d", h=4), sS[j].rearrange("h a b -> a h b"), writes=[dtss])
            dve("tensor_copy", [dtss], [dSS[cur]], out=SS[:, cur], in_=tss[:, :].rearrange("p (h d) -> p h d", h=4))
            k.dma(SP, M0B[:, :], sm[j].partition_broadcast(128), writes=[dM0B])
            pool("memset", writes=[dM4], ap=M4[:, :], constant=0.0)
            k.dma(SP, M4[0:4, 0:1], sm[j].rearrange("(h o) -> h o", o=1), writes=[dM4])
            pool("memset", writes=[dBBC], ap=BBC[:, :], constant=0.0)
            k.act(M0B[:, :], M0B[:, :], AF.Exp, [dM0B], [dM0B])
            dve("tensor_tensor", [dSTG, dM0B], [dCA[cur]], out=CA[:, cur, :, 0:129], in0=STG[:, :, :],
                in1=M0B[:, :].unsqueeze(2).to_broadcast([128, 4, 129]), op=ALU.mult)
            dve("tensor_scalar_mul", [dconst], [dCA[cur]], out=CA[:, cur, :, 129:130], scalar1=0.0,
                in0=ones[:, 0:4].unsqueeze(2))

    def seq_final(seg):
        cur = state["cur"]
        j = seg.j
        oC, on, om, oS, ocv = (o_pC, o_pn, o_pm, o_pS, o_pcv) if seg.kind == "p" else (o_sC, o_sn, o_sm, o_sS, o_scv)
        dve("tensor_tensor", [dM4], [dM4], out=M4[:, 4:5], in0=M4[:, 0:1], in1=M4[:, 1:2], op=ALU.add)
        k.dma(SP, om[j].rearrange("(h o) -> h o", o=1), M4[:, 4:5], reads=[dM4])
        dve("tensor_scalar_mul", [dM4, dconst], [dDG], out=DG[:, :], in0=ident[0:4, 0:4], scalar1=M4[:, 4:5])
        p, dp = ps()
        k.mm(p[:, 0:4], ones[0:4, :], DG[:, :], True, True, [dDG, dconst], [dp])
        k.act(M0B[:, :], p[:, 0:4], AF.Exp, [dp], [dM0B], scale=-1.0)
        dve("tensor_tensor", [dCA[cur], dM0B], [dSTG], out=STG[:, :, :], in0=CA[:, cur, :, 0:129],
            in1=M0B[:, :].unsqueeze(2).to_broadcast([128, 4, 129]), op=ALU.mult)
        k.dma(SP, oC[j].rearrange("h a b -> a h b"), STG[:, :, 0:128], reads=[dSTG])
        store_T(on[j], STG[:, :, 128], 4, [dSTG])
        k.dma(SP, oS[j].rearrange("h a b -> a h b"), SS[:, cur].bitcast(F32), reads=[dSS[cur]])
        dve("tensor_copy", dHIST[seg.sidx], [dHS], out=HS[:, :].rearrange("p (t c) -> p t c", t=3),
            in_=HIST4[:, seg.sidx].rearrange("p c t -> p t c"))
        store_T(ocv[j].rearrange("t (c p) -> (t c) p", p=128), HS[:, :], 36, [dHS])

    PTF = [PT[0][:].bitcast(F32), PT[1][:].bitcast(F32)]
    pre_out = {}

    def prelude_gen(i, t):
        L, col = t.L, t.col
        par = i % 2
        SMt = SM[:, 128 * par:128 * par + 128]
        dSg_ = dSgp[par]
        S_ = [dSg_]
        smc = lambda a_, b_, L_: SMt[:L_, a_:b_]
        G1 = smc(0, 16, L)
        EX = smc(16, 28, L)
        L2 = smc(28, 36, L)
        LG = smc(36, 44, L)
        BETA = smc(44, 48, L)
        NBETA = smc(48, 52, L)
        CS = smc(52, 60, L)
        A_ = smc(60, 64, L)
        GAM = smc(64, 68, L)
        DL = smc(68, 72, L)
        WS = smc(72, 76, L)
        DEN = smc(76, 80, L)
        RINV = smc(80, 84, L)
        SSQ = smc(84, 88, L)
        SCL = smc(88, 92, L)
        TMP4 = smc(92, 96, L)
        SSQ2 = smc(96, 100, L)
        SCL2 = smc(100, 104, L)
        NG = smc(104, 108, L)
        dve("tensor_tensor", [dGT[i], dconst], S_, out=G1, in0=GT[:L, i, :], in1=GB[:L, :], op=ALU.add)
        yield
        dve("tensor_tensor", S_ + [dconst], S_, out=G1, in0=G1, in1=SGN[:L, :], op=ALU.mult)
        yield
        k.act(EX, SMt[:L, 4:16], AF.Exp, S_, S_)
        yield
        k.act(L2, SMt[:L, 16:24], AF.Ln, S_, S_, bias=1.0)
        yield
        dve("tensor_scalar_add", S_, S_, out=BETA, in0=SMt[:L, 24:28], scalar1=1.0)
        yield
        dve("reciprocal", S_, S_, out=BETA, in_=BETA)
        yield
        dve("tensor_scalar_mul", S_, S_, out=NBETA, in0=BETA, scalar1=-1.0)
        yield
        dve("tensor_tensor", S_ + [dconst], S_, out=LG, in0=L2, in1=NEG8[:L, :], op=ALU.mult)
        yield
        p, dp = PTF[0], ptd[0]
        k.mm(p[:L, 0:8], tri[:L, :L], LG, True, True, S_ + [dconst], [dp])
        yield
        dve("tensor_copy", [dp], S_, out=CS, in_=p[:L, 0:8])
        yield
        dve("tensor_tensor", S_ + [dconst], [dRBd], out=RB[:L, 0:4, :L],
            in0=tri[:L, :L].unsqueeze(1).to_broadcast([L, 4, L]),
            in1=SMt[:L, 36:40].unsqueeze(2).to_broadcast([L, 4, L]), op=ALU.mult)
        yield
        pool("tensor_tensor", S_ + [dconst], [dRBg], out=RB[:L, 4:8, :L],
             in0=tri[:L, :L].unsqueeze(1).to_broadcast([L, 4, L]),
             in1=SMt[:L, 40:44].unsqueeze(2).to_broadcast([L, 4, L]), op=ALU.mult)
        yield
        pb, dpb = PTF[1], ptd[1]
        pg, dpg = PTF[0], ptd[0]
        pbv = pb[:, 0:4 * L].rearrange("p (h t) -> p h t", h=4)
        pgv = pg[:, 0:4 * L].rearrange("p (h t) -> p h t", h=4)
        k.mm(pbv, ones[:L, :], RB[:L, 0:4, :L], True, True, [dRBd, dconst], [dpb])
        yield
        k.mm(pgv, ones[:L, :], RB[:L, 4:8, :L], True, True, [dRBg, dconst], [dpg])
        yield
        dve("tensor_tensor", S_, S_, out=A_, in0=SMt[:L, 0:4], in1=SMt[:L, 52:56], op=ALU.subtract)
        yield
        dve("tensor_scalar_mul", S_, S_, out=NG, in0=SMt[:L, 56:60], scalar1=-1.0)


        yield
        pre_out[i] = (pbv, dpb, pgv, dpg)
        yield

    def mixer_tile(i, t, nxt_t=None):
        L, col = t.L, t.col
        par = i % 2
        cur = state["cur"]
        nxt = 1 - cur
        SMt = SM[:, 128 * par:128 * par + 128]
        dSg_ = dSgp[par]
        BLGt, dBLGt = BLG2[par], dBLGp[par]
        S_ = [dSg_]
        smc = lambda a_, b_, L_: SMt[:L_, a_:b_]
        if i not in pre_out:
            for _ in prelude_gen(i, t):
                pass
        pbv, dpb, pgv, dpg = pre_out.pop(i)
        G1 = smc(0, 16, L)
        EX = smc(16, 28, L)
        L2 = smc(28, 36, L)
        LG = smc(36, 44, L)
        BETA = smc(44, 48, L)
        NBETA = smc(48, 52, L)
        CS = smc(52, 60, L)
        A_ = smc(60, 64, L)
        GAM = smc(64, 68, L)
        DL = smc(68, 72, L)
        WS = smc(72, 76, L)
        DEN = smc(76, 80, L)
        RINV = smc(80, 84, L)
        SSQ = smc(84, 88, L)
        SCL = smc(88, 92, L)
        TMP4 = smc(92, 96, L)
        SSQ2 = smc(96, 100, L)
        SCL2 = smc(100, 104, L)
        NG = smc(104, 108, L)
        dve("tensor_copy", [dpb], [dBLGt], out=BLGt[:, 0:4], in_=pbv[:, :, L - 1])
        dve("tensor_copy", [dpg], [dBLGt], out=BLGt[:, 4:8], in_=pgv[:, :, L - 1])
        EG, dEG = TT[2], dTT[2]
        EGv = EG[:, 0:4 * L].rearrange("p (h t) -> p h t", h=4)
        k.act(EGv, pgv, AF.Exp, [dpg], [dEG])
        DT_, dDT = TT[3], dTT[3]
        DTv = DT_[:L, 0:4 * L].rearrange("p (h t) -> p h t", h=4)
        for h in range(4):
            dve("tensor_scalar", [dpg] + S_, [dDT], out=DTv[:, h, :], in0=pgv[:L, h, :], scalar1=SMt[:L, 104 + h:105 + h],
                scalar2=0.0, op0=ALU.add, op1=ALU.min)
        def gen_a():
            Sr = [dSg_, dSa]
            S_ = [dSa]
            E1, dE1 = TT[0], dTT[0]
            E1v = E1[:, 0:4 * L].rearrange("p (h t) -> p h t", h=4)
            k.act(E1v, pbv, AF.Exp, [dpb], [dE1])
            yield
            WT, dWT = TT[1], dTT[1]
            WTv = WT[:L, 0:4 * L].rearrange("p (h t) -> p h t", h=4)
            for h in range(4):
                k.act(WTv[:, h, :], pbv[:L, h, :], AF.Exp, [dpb] + Sr, [dWT], bias=SMt[:L, 60 + h:61 + h])
            yield
            pool("tensor_tensor", [dWT, dconst], [dWT], out=WTv, in0=WTv,
                 in1=maskA[:L, :L].unsqueeze(1).to_broadcast([L, 4, L]), op=ALU.mult)
            yield
            QS, dQS = QSR, dQSR
            QSv = QS[:, 0:4 * L].rearrange("p (h t) -> p h t", h=4)
            dve("tensor_tensor", dQKA[0:4] + [dE1], [dQS], out=QSv, in0=QKA[:, 0:4, col:col + L], in1=E1v, op=ALU.mult)
            yield
            pst, dpst = psA()
            pstv = pst[:L, 0:4 * L].rearrange("p (h t) -> p h t", h=4)
            for h in range(4):
                k.mm(pstv[:, h, :], QKA[:, 4 + h, col:col + L], QKA[:, h, col:col + L], True, True,
                     [dQKA[h], dQKA[4 + h]], [dpst], inc=(h == 3))
            yield
            SMK, dSMK = SMKR, dSMKR
            SMKv = SMK[:L, 0:4 * L].rearrange("p (h t) -> p h t", h=4)
            dve("tensor_tensor", [dpst, dWT], [dSMK], out=SMKv, in0=pstv, in1=WTv, op=ALU.mult)
            yield
            dve("tensor_tensor", Sr + [dBLGt], S_, out=WS, in0=A_, in1=BLGt[:L, 0:4], op=ALU.add)
            yield
            k.act(WS, WS, AF.Exp, Sr, S_, bias=-0.5 * float(np.log(128.0)))
            yield
            dKW = dKWd
            pool("tensor_tensor", [dKA[i]] + Sr, [dKW], out=KW[:L, :, :],
                 in0=KA[:L, i, :].rearrange("p (h d) -> p h d", h=4),
                 in1=WS.unsqueeze(2).to_broadcast([L, 4, 128]), op=ALU.mult)
            yield
            pn_ = []
            for half in range(2):
                p, dp = psA()
                pv = p[:L, 0:260].rearrange("p (h d) -> p h d", h=2)
                for hh in range(2):
                    h = half * 2 + hh
                    k.mm(pv[:, hh, :], QSv[:, h, :], CA[:, cur, h, :], True, False, [dQS, dCA[cur]], [dp], inc=False)
                    k.mm(pv[:, hh, :], SMKv[:, h, :], VA[:L, i, h, :], False, True, [dSMK, dVA[i]], [dp], inc=True)
                pn_.append((pv, dp))
            yield
            for half in range(2):
                pv, dp = pn_[half]
                dve("tensor_copy", [dp], S_, out=SMt[:L, 76 + 2 * half:78 + 2 * half], in_=pv[:, :, 128])
                for hh in range(2):
                    h = half * 2 + hh
                    k.act(JUNK[:L, 0:128], pv[:, hh, 0:128], AF.Square, [dp], S_, accum=SMt[:L, 84 + h:85 + h])
            yield
            dve("scalar_tensor_tensor", Sr, S_, out=RINV, in0=DEN, scalar=-1.0, in1=DEN, op0=ALU.mult, op1=ALU.max)
            yield
            dve("tensor_scalar_max", Sr, S_, out=RINV, in0=RINV, scalar1=1.0)
            yield
            dve("reciprocal", Sr, S_, out=RINV, in_=RINV)
            yield
            dve("tensor_tensor", Sr, S_, out=TMP4, in0=RINV, in1=RINV, op=ALU.mult)
            yield
            dve("tensor_tensor", Sr, S_, out=TMP4, in0=TMP4, in1=SSQ, op=ALU.mult)
            yield
            k.act(TMP4, TMP4, AF.Ln, Sr, S_, bias=EPS, scale=1.0 / 128)
            yield
            k.act(TMP4, TMP4, AF.Exp, Sr, S_, scale=-0.5)
            yield
            dve("tensor_tensor", Sr, S_, out=SCL, in0=TMP4, in1=RINV, op=ALU.mult)
            yield
            for half in range(2):
                pv, dp = pn_[half]
                for hh in range(2):
                    h = half * 2 + hh
                    dve("scalar_tensor_tensor", [dp, dOG[i]] + Sr, [dH[0]], out=H[:L, 0, h * 128:(h + 1) * 128],
                        in0=pv[:, hh, 0:128], scalar=SMt[:L, 88 + h:89 + h], in1=OG[:L, i, h * 128:(h + 1) * 128],
                        op0=ALU.mult, op1=ALU.mult)
            yield
            for half in range(2):
                p, dp = psA()
                pv = p[:, 0:260].rearrange("p (h d) -> p h d", h=2)
                for hh in range(2):
                    h = half * 2 + hh
                    k.mm(pv[:, hh, :], KW[:L, h, :], VA[:L, i, h, :], True, True, [dKW, dVA[i]], [dp], inc=(hh == 1))
                for hh in range(2):
                    h = half * 2 + hh
                    dve("scalar_tensor_tensor", [dp, dCA[cur], dE1], [dCA[nxt]], out=CA[:, nxt, h, :], in0=CA[:, cur, h, :],
                        scalar=E1v[:, h, L - 1:L], in1=pv[:, hh, :], op0=ALU.mult, op1=ALU.add)
            yield
            dve("tensor_tensor", Sr + [dBBC], [dAGd], out=AG[:L, 0:4], in0=A_, in1=BBC[:L, :], op=ALU.subtract)
            yield
            dve("tensor_copy", Sr, [dAGd], out=AG[:L, 4:8], in_=SMt[:L, 36:40])
            yield
            p, dp = psA()
            k.tr(p[0:4, 0:L], AG[:L, 0:4], ident[:L, :L], [dAGd, dconst], [dp], inc=False)
            k.tr(p[0:4, 128:128 + L], AG[:L, 4:8], ident[:L, :L], [dAGd, dconst], [dp])
            dve("reduce_max", [dp], [dM4], out=M4[:, 2:3], in_=p[0:4, 0:L], axis=AX.X)
            yield
            dve("tensor_tensor", [dM4], [dM4], out=M4[:, 0:1], in0=M4[:, 0:1], in1=M4[:, 2:3], op=ALU.max)
            yield
            dve("reduce_sum", [dp], [dM4], out=M4[:, 3:4], in_=p[0:4, 128:128 + L], axis=AX.X)
            yield
            dve("tensor_tensor", [dM4], [dM4], out=M4[:, 1:2], in0=M4[:, 1:2], in1=M4[:, 3:4], op=ALU.add)
            yield
            dve("tensor_tensor", [dBBC, dBLGt], [dBBC], out=BBC[:, :], in0=BBC[:, :], in1=BLGt[:, 0:4], op=ALU.add)


            yield
        def gen_b():
            Sr = [dSg_, dSb]
            S_ = [dSb]
            k.act(DTv, DTv, AF.Exp, [dDT], [dDT])
            yield
            DMS, dDMS = TT[4], dTT[4]
            DMSv = DMS[:L, 0:4 * L].rearrange("p (h t) -> p h t", h=4)
            pool("tensor_tensor", [dDT, dconst], [dDMS], out=DMSv, in0=DTv,
                 in1=mstrict[:L, :L].unsqueeze(1).to_broadcast([L, 4, L]), op=ALU.mult)
            yield
            pool("tensor_tensor", [dDT, dconst], [dDT], out=DTv, in0=DTv,
                 in1=tri[:L, :L].unsqueeze(1).to_broadcast([L, 4, L]), op=ALU.mult)
            yield
            k.act(GAM, SMt[:L, 56:60], AF.Exp, Sr, S_)
            yield
            dve("tensor_tensor", Sr + [dBLGt], S_, out=DL, in0=NG, in1=BLGt[:L, 4:8], op=ALU.add)
            yield
            k.act(DL, DL, AF.Exp, Sr, S_)
            yield
            pk, dpk = psB()
            pkv = pk[:L, :].rearrange("p (h d) -> p h d", h=4)
            for h in range(4):
                k.tr(pkv[:, h, :], POSTF[:, 4 + h, col:col + L], ident[:, :], [dPOST[4 + h], dconst], [dpk], inc=(h == 3))
            yield
            pool_or_dve = dve
            dve("tensor_tensor", [dpk] + Sr, [dGKd], out=GK[:L, :, :], in0=pkv, in1=GAM.unsqueeze(2).to_broadcast([L, 4, 128]),
                op=ALU.mult)
            yield
            dve("tensor_tensor", [dpk] + Sr, [dKDd], out=KD[:L, :, :], in0=pkv, in1=DL.unsqueeze(2).to_broadcast([L, 4, 128]),
                op=ALU.mult)
            yield
            pv_, dpv = psB()
            pvv = pv_[:L, :].rearrange("p (h d) -> p h d", h=4)
            for h in range(4):
                k.tr(pvv[:, h, :], POSTF[:, 8 + h, col:col + L], ident[:, :], [dPOST[8 + h], dconst], [dpv], inc=(h == 3))
            yield
            k.act(VB[:L, :, :], pvv, AF.Copy, [dpv], [dVBd])
            yield
            pkk, dpkk = psB()
            pkkv = pkk[:L, 0:4 * L].rearrange("p (h t) -> p h t", h=4)
            for h in range(4):
                k.mm(pkkv[:, h, :], POST[:, 4 + h, col:col + L], POST[:, 4 + h, col:col + L], True, True, [dPOST[4 + h]],
                     [dpkk], inc=(h == 3))
            yield
            ny = 0
            for h in range(4):
                dve("scalar_tensor_tensor", [dpkk, dDMS] + Sr, [dNYFh[h // 2]], out=NYF[:L, h, :L], in0=pkkv[:, h, :],
                    scalar=SMt[:L, 48 + h:49 + h], in1=DMSv[:, h, :], op0=ALU.mult, op1=ALU.mult)
            yield
            pqk, dpqk = psB()
            pqkv = pqk[:L, 0:4 * L].rearrange("p (h t) -> p h t", h=4)
            for h in range(4):
                k.mm(pqkv[:, h, :], POST[:, 4 + h, col:col + L], POST[:, h, col:col + L], True, True,
                     [dPOST[4 + h], dPOST[h]], [dpqk], inc=(h == 3))
            yield
            QKM, dQKM = QKMR, dQKMR
            QKMv = QKMR[:L, 0:4 * L].rearrange("p (h t) -> p h t", h=4)
            dve("tensor_tensor", [dpqk, dDT], [dQKM], out=QKMv, in0=pqkv, in1=DTv, op=ALU.mult)
            yield
            QG, dQG = QGR, dQGR
            QGv = QGR[:, 0:4 * L].rearrange("p (h t) -> p h t", h=4)
            dve("tensor_tensor", dPOST[0:4] + [dEG, dQKM], [dQG], out=QGv, in0=POST[:, 0:4, col:col + L], in1=EGv, op=ALU.mult)
            yield
            npi = 0
            nlev = int(np.log2(L)) - 1

            def half_chain(hf):
                h0 = 2 * hf
                hs = (h0, h0 + 1)
                dY, dZ, dP = dNYFh[hf], dNZFh[hf], dNPh[hf]
                p, dp = psB()
                pzv = p[:L, 0:2 * L].rearrange("p (h t) -> p h t", h=2)
                for a, h in enumerate(hs):
                    k.tr(pzv[:, a, :], NYF[:L, h, :L].bitcast(F32), ident[:L, :L], [dY, dconst], [dp], inc=(a == 1))
                yield
                k.act(NZF[:L, 0, h0:h0 + 2, :L], pzv, AF.Copy, [dp], [dZ[0]])
                yield
                pool("tensor_tensor", [dY, dconst], [dP], out=NP_[:L, npi, h0:h0 + 2, :L], in0=NYF[:L, h0:h0 + 2, :L],
                     in1=ident[:L, :L].unsqueeze(1).to_broadcast([L, 2, L]), op=ALU.add)
                yield
                zf = 0
                for lev in range(nlev):
                    last = (lev == nlev - 1)
                    p, dp = psB()
                    pzv = p[:L, 0:2 * L].rearrange("p (h t) -> p h t", h=2)
                    for a, h in enumerate(hs):
                        k.mm(pzv[:, a, :], NYF[:L, h, :L], NZF[:L, zf, h, :L], True, True, [dY, dZ[zf]], [dp], inc=(a == 1))
                    yield
                    k.act(NZF[:L, 1 - zf, h0:h0 + 2, :L], pzv, AF.Copy, [dp], [dZ[1 - zf]])
                    yield
                    if not last:
                        p2, dp2 = psB()
                        pyv = p2[:L, 0:2 * L].rearrange("p (h t) -> p h t", h=2)
                        for a, h in enumerate(hs):
                            k.mm(pyv[:, a, :], NZF[:L, zf, h, :L], NYF[:L, h, :L], True, True, [dY, dZ[zf]], [dp2],
                                 inc=(a == 1))
                        yield
                        dve("tensor_copy", [dp2], [dY], out=NYF[:L, h0:h0 + 2, :L], in_=pyv)
                        yield
                    p3, dp3 = psB()
                    ppv = p3[:L, 0:2 * L].rearrange("p (h t) -> p h t", h=2)
                    for a, h in enumerate(hs):
                        k.mm(ppv[:, a, :], NZF[:L, 1 - zf, h, :L], NP_[:L, npi, h, :L], True, True, [dZ[1 - zf], dP], [dp3],
                             inc=(a == 1))
                    yield
                    zf = 1 - zf
                    dve("tensor_tensor", [dp3, dP], [dP], out=NP_[:L, npi, h0:h0 + 2, :L], in0=ppv,
                        in1=NP_[:L, npi, h0:h0 + 2, :L], op=ALU.add)
                    yield

            subs = [half_chain(0), half_chain(1)]
            while subs:
                for s_ in list(subs):
                    try:
                        next(s_)
                    except StopIteration:
                        subs.remove(s_)
                yield
            Q_ = NP_[:L, npi]
            dQ = dNPh[0]
            dQ2 = dNPh[1]
            p, dp = psB()
            puv = p[:L, :].rearrange("p (h d) -> p h d", h=4)
            for h in range(4):
                k.mm(puv[:, h, :], Q_[:, h, :L], VB[:L, h, :], True, True, [dQ, dQ2, dVBd], [dp], inc=(h == 3))
            yield
            dve("tensor_tensor", [dp] + Sr, [dBUd], out=BU[:L, :, :], in0=puv, in1=BETA.unsqueeze(2).to_broadcast([L, 4, 128]),
                op=ALU.mult)
            yield
            p, dp = psB()
            pwv = p[:, 0:4 * L].rearrange("p (h t) -> p h t", h=4)
            for h in range(4):
                k.mm(pwv[:, h, :], GK[:L, h, :], Q_[:, h, :L], True, True, [dQ, dQ2, dGKd], [dp], inc=(h == 3))
            yield
            k.act(WKT[:, :, :L], pwv, AF.Copy, [dp], [dWKTd])
            yield
            p, dp = psB()
            p1v = p[:L, :].rearrange("p (h d) -> p h d", h=4)
            for h in range(4):
                k.mm(p1v[:, h, :], WKT[:, h, :L], SS[:, cur, h, :], True, True, [dWKTd, dSS[cur]], [dp], inc=(h == 3))
            yield
            WV = BU
            for h in range(4):
                dve("scalar_tensor_tensor", [dp, dBUd] + Sr, [dWVd], out=BU[:L, h, :], in0=p1v[:, h, :],
                    scalar=SMt[:L, 48 + h:49 + h], in1=BU[:L, h, :], op0=ALU.mult, op1=ALU.add)
            yield
            po, dpo = psB()
            pov = po[:L, :].rearrange("p (h d) -> p h d", h=4)
            for h in range(4):
                k.mm(pov[:, h, :], QGv[:, h, :], SS[:, cur, h, :], True, False, [dQG, dSS[cur]], [dpo], inc=False)
                k.mm(pov[:, h, :], QKMv[:, h, :], WV[:L, h, :], False, True, [dQKM, dWVd], [dpo], inc=True)
            yield
            p, dp = psB()
            psv = p[:, :].rearrange("p (h d) -> p h d", h=4)
            for h in range(4):
                k.mm(psv[:, h, :], KD[:L, h, :], WV[:L, h, :], True, True, [dKDd, dWVd], [dp], inc=(h == 3))
            yield
            dve("tensor_tensor", [dSS[cur], dEG], [dSS[nxt]], out=SS[:, nxt], in0=SS[:, cur],
                in1=EGv[:, :, L - 1:L].to_broadcast([128, 4, 128]), op=ALU.mult)
            yield
            dve("tensor_tensor", [dSS[nxt], dp], [dSS[nxt]], out=SS[:, nxt], in0=SS[:, nxt], in1=psv, op=ALU.add)
            yield
            for h in range(4):
                k.act(JUNK[:L, 0:128], pov[:, h, :], AF.Square, [dpo], S_, accum=SMt[:L, 96 + h:97 + h])
            yield
            k.act(SSQ2, SSQ2, AF.Ln, Sr, S_, bias=EPS, scale=1.0 / 128)
            yield
            k.act(SCL2, SSQ2, AF.Exp, Sr, S_, scale=-0.5)
            yield
            for h in range(4):
                dve("scalar_tensor_tensor", [dpo, dZG[i]] + Sr, [dH[0]], out=H[:L, 0, 512 + h * 128:512 + (h + 1) * 128],
                    in0=pov[:, h, :], scalar=SMt[:L, 100 + h:101 + h], in1=ZG[:L, i, h * 128:(h + 1) * 128],
                    op0=ALU.mult, op1=ALU.mult)

            yield
        gens_ = [gen_a(), gen_b()]
        if nxt_t is not None:
            gens_.append(prelude_gen(i + 1, nxt_t))
        interleave(gens_, 3)
        state["cur"] = nxt
        p, dp = PS[0][:].bitcast(BF16), psd[0]
        pv = p.rearrange("p (k c) -> p k c", k=8)
        for kc in range(8):
            k.tr(pv[:, kc, :L], H[:L, 0, kc * 128:(kc + 1) * 128], identb[:L, :L], [dH[0], dconst], [dp], inc=(kc == 7))
        k.act(HT[:, :, col:col + L], pv[:, :, :L], AF.Copy, [dp], [dHT])

    dRBd, dKWd, dAGd, dGKd, dKDd, dVBd, dBUd, dWKTd, dWVd, dDG, dSTG, dM0B, dBLG = [Dep() for _ in range(13)]
    dNY = [Dep(), Dep()]
    dNZ = [Dep(), Dep()]
    dNP = [Dep(), Dep()]

    def add_to_x(p, dp, i, L, cb):
        dve("tensor_tensor", [dp, dX[i]], [dX[i]], out=X[:L, i, cb * 512:(cb + 1) * 512], in0=X[:L, i, cb * 512:(cb + 1) * 512],
            in1=p[:L, :], op=ALU.add)

    def proj_resid(name, tiles, src, dsrc, norm_idx):
        ws = [wblk(name, 0), wblk(name, 1, ahead=1)]
        for i, t in enumerate(tiles):
            for cb in range(2):
                w, dw = ws[cb]
                p, dp = ps()
                for kc in range(8):
                    k.mm(p[:t.L, :], src[:, kc, t.col:t.col + t.L], w[:, kc, :], kc == 0, kc == 7, [dsrc, dw], [dp])
                add_to_x(p, dp, i, t.L, cb)
            if i >= 1:
                ti = tiles[i - 1]
                rmsnorm_to_xnt(X[:ti.L, i - 1, :], dX[i - 1], ti.L, ti.col, norm_idx, (i - 1) % 2)
        ti = tiles[-1]
        rmsnorm_to_xnt(X[:ti.L, len(tiles) - 1, :], dX[len(tiles) - 1], ti.L, ti.col, norm_idx, (len(tiles) - 1) % 2)

    dQF, dPTS, dOTB, dPF, dPN, dPF2, dPN2 = [Dep() for _ in range(7)]
    dSat = [Dep() for _ in range(4)]
    PF2 = ARM[:, 0:1024]
    PN2 = ARM[:, 1024:1536].bitcast(BF16)

    def phase_attn(tiles, segs, NT):
        for cb in range(2):
            w, dw = wblk("q", cb)
            for jj in range(4):
                p, dp = proj_fm(w, dw, jj, NT)
                k.act(QF[:, cb * 4 + jj, :NT], p[:, :NT], AF.Copy, [dp], [dQF] + dQKA[0:4])
        def attn_tile(i, t):
            L, col = t.L, t.col
            par = i % 2
            PFt, PNt, dPFt, dPNt = (PF, PN, dPF, dPN) if par == 0 else (PF2, PN2, dPF2, dPN2)
            c0 = 128 + 32 * i
            S_ = [dSat[i]]
            while len(free_ps) < 2:
                yield
            bks = [free_ps.pop(0), free_ps.pop(0)]
            pp = []
            for half in range(2):
                p, dp = PS[bks[half]], psd[bks[half]]
                pv = p[:L, :].rearrange("p (h n) -> p h n", h=2)
                for hh in range(2):
                    h = half * 2 + hh
                    for dc in range(2):
                        k.mm(pv[:, hh, :], QF[:, 2 * h + dc, col:col + L], KT[:, 2 * h + dc, :], dc == 0, dc == 1,
                             [dQF, dKT], [dp], inc=(hh == 1 and dc == 1))
                pp.append((pv, dp))
            yield
            for half in range(2):
                pv, dp = pp[half]
                dve("reduce_max", [dp], S_, out=SM[:L, c0 + 2 * half:c0 + 2 + 2 * half], in_=pv, axis=AX.X)
            dve("tensor_scalar_mul", S_, S_, out=SM[:L, c0 + 4:c0 + 8], in0=SM[:L, c0:c0 + 4], scalar1=-1.0 / 16)
            yield
            PFv = PFt[:L, :].rearrange("p (h n) -> p h n", h=4)
            for half in range(2):
                pv, dp = pp[half]
                for hh in range(2):
                    h = half * 2 + hh
                    k.act(PFv[:, h, :], pv[:, hh, :], AF.Exp, [dp] + S_, [dPFt] + (dOG if par == 0 else dPOST[0:2]),
                          bias=SM[:L, c0 + 4 + h:c0 + 5 + h],
                          scale=1.0 / 16, accum=SM[:L, c0 + 8 + h:c0 + 9 + h])
            free_ps.extend(bks)
            yield
            dve("reciprocal", S_ + [dPFt], S_, out=SM[:L, c0 + 12:c0 + 16], in_=SM[:L, c0 + 8:c0 + 12])
            dve("tensor_tensor", [dPFt] + S_, [dPNt] + (dZG if par == 0 else dPOST[2:3]),
                out=PNt[:L, :].rearrange("p (h n) -> p h n", h=4), in0=PFv,
                in1=SM[:L, c0 + 12:c0 + 16].unsqueeze(2).to_broadcast([L, 4, 256]), op=ALU.mult)
            yield
            p, dp = pt()
            pv = p[:].rearrange("p (k c) -> p k c", k=8)
            for c8 in range(8):
                k.tr(pv[:, c8, :L], PNt[:L, c8 * 128:(c8 + 1) * 128], identb[:L, :L], [dPNt, dconst], [dp], inc=(c8 == 7))
            yield
            k.act(PTS[:, :, col:col + L], pv[:, :, :L], AF.Copy, [dp], [dPTS] + dQKA[4:8])
            yield

        for seg in segs:
            if seg.kind == "s":
                sample_kv(seg.j)
            interleave([attn_tile(i, t) for i, t in enumerate(tiles) if t.seg is seg], 2)
            assert len(free_ps) == 6
            for h in range(4):
                for dc in range(2):
                    p, dp = ps()
                    for ncx in range(2):
                        k.mm(p[:, :seg.n], VV[:, ncx, h * 256 + dc * 128:h * 256 + (dc + 1) * 128],
                             PTS[:, h * 2 + ncx, seg.col0:seg.col0 + seg.n], ncx == 0, ncx == 1, [dVV, dPTS], [dp])
                    k.act(OTB[:, h * 2 + dc, seg.col0:seg.col0 + seg.n], p[:, :seg.n], AF.Copy, [dp], [dOTB] + dKA)
        proj_resid("o", tiles, OTB, dOTB, 2)

    dHID = [Dep() for _ in range(32)]

    def phase_ffn(tiles, NT, next_tiles=None):
        if next_tiles is not None:
            prefetch_x(next_tiles)
        for b in range(8):
            w, dw = wblk("1", b)
            for jj in range(4):
                f = b * 4 + jj
                p, dp = proj_fm(w, dw, jj, NT)
                tt, dtt = tmp()
                k.act(tt[:, :NT], p[:, :NT], AF.Relu, [dp], [dtt])
                pool("tensor_tensor", [dtt], [dHID[f]], out=HID[:, f, :NT], in0=tt[:, :NT], in1=tt[:, :NT], op=ALU.mult)
        for cb in range(2):
            acc = [ps() for _ in tiles]
            for fb in range(4):
                w, dw = wblk("2", cb * 4 + fb)
                for i, t in enumerate(tiles):
                    p, dp = acc[i]
                    for f8 in range(8):
                        f = fb * 8 + f8
                        k.mm(p[:t.L, :], HID[:, f, t.col:t.col + t.L], w[:, f8, :], fb == 0 and f8 == 0,
                             fb == 3 and f8 == 7, [dHID[f], dw], [dp], inc=(f8 == 7))
                if next_tiles is not None:
                    blk_i = cb * 4 + fb
                    al = lambda i_: [dXP[i_]] + dKA + dOG + dZG
                    if blk_i == 0:
                        for i_ in (0, 1):
                            norm_pre(XP[:128, i_, :], dXP[i_], 128, i_ % 2, al(i_))
                    elif blk_i == 1:
                        for i_ in (0, 1):
                            norm_tr(128, next_tiles[i_].col, 0, i_ % 2)
                        for i_ in (2, 3):
                            norm_pre(XP[:128, i_, :], dXP[i_], 128, i_ % 2, al(i_))
                    elif blk_i == 2:
                        for i_ in (2, 3):
                            norm_tr(128, next_tiles[i_].col, 0, i_ % 2)
            for i, t in enumerate(tiles):
                p, dp = acc[i]
                add_to_x(p, dp, i, t.L, cb)
        for i, t in enumerate(tiles):
            L = t.L
            par = i % 2
            d_ = dSTF[par]
            k.act(JUNK[:L, :], X[:L, i, :], AF.Square, [dX[i]], [d_], accum=ST[:L, 16 + par:17 + par])
            k.act(ST[:L, 18 + par:19 + par], ST[:L, 16 + par:17 + par], AF.Ln, [d_], [d_], bias=EPS, scale=1.0 / D)
            k.act(ST[:L, 20 + par:21 + par], ST[:L, 18 + par:19 + par], AF.Exp, [d_], [d_], scale=-0.5)
            dve("scalar_tensor_tensor", [dX[i], d_, dconst], [dX[i]], out=X[:L, i, :], in0=X[:L, i, :],
                scalar=ST[:L, 20 + par:21 + par], in1=WBF[:L, :], op0=ALU.mult, op1=ALU.mult)
            k.dma(SP, t.dst, X[:L, i, :], reads=[dX[i]])

    def run_group(tiles, segs, NT, pref=False, next_tiles=None):
        for seg in segs:
            if seg.first:
                hist_init(seg)
        phase_in(tiles, segs, NT, pref)
        for i, t in enumerate(tiles):
            seg = t.seg
            if seg.first and t.col == seg.col0:
                state_init(seg)
            mixer_tile(i, t, tiles[i + 1] if i + 1 < len(tiles) else None)
            if seg.last and t.col + t.L == seg.col0 + seg.n:
                seq_final(seg)
        proj_resid("out", tiles, HT, dHT, 1)
        k.barrier()
        phase_attn(tiles, segs, NT)
        k.barrier()
        phase_ffn(tiles, NT, next_tiles)
        k.barrier()

    if stage == 0:
        pass
    elif stage < 2:
        mem_kv(0)
        mem_kv(1)
    else:
        tiles, segs = [], []
        for j in range(NSS):
            seg = Seg("s", j, j * LS, LS, True, True, sidx=j)
            segs.append(seg)
            tiles.append(Tile(LS, j * LS, xs[j], ys[j], seg))
        run_group(tiles, segs, NSS * LS)
        for j in range(NPS):
            mem_kv(j)
            def mk(g):
                seg = Seg("p", j, 0, 512, g == 0, g == KNG - 1, sidx=0)
                return seg, [Tile(128, i * 128, xp[j, g * 512 + i * 128:g * 512 + (i + 1) * 128, :],
                                  yp[j, g * 512 + i * 128:g * 512 + (i + 1) * 128, :], seg) for i in range(4)]
            for g in range(KNG):
                seg, tiles = mk(g)
                nxt_tiles = mk(g + 1)[1] if g + 1 < KNG else None
                run_group(tiles, [seg], 512, pref=(g > 0), next_tiles=nxt_tiles)
    print("emitted instructions:", k.nins, "waits:", k.nwait, "dmas:", k.dma_i)

    for i in range(ND):
        if k.dcnt[i] > 0:
            k.wait(SP, (k.dsem[i], k.dcnt[i]))
    for i in range(16):
        if k.wcnt[i] > 0:
            k.wait(SP, (k.wsem[i], k.wcnt[i]))
    for E in k.engs:
        if E.cnt > 0:
            k.wait(SP, (E.sem, E.cnt))
    k.es.close()
    return k


def _shard(inputs):
    f = lambda a: np.ascontiguousarray(np.asarray(a, dtype=np.float32))
    maps = []
    for c in range(NCORES):
        p0, s0 = c * NPS, c * NSS
        m = {
            "xp": f(inputs["x_prompt"][p0:p0 + NPS]),
            "xs": f(inputs["x_sample"][s0:s0 + NSS]),
            "sC": f(inputs["state_mlstm_C"][0, s0:s0 + NSS]),
            "sn": f(inputs["state_mlstm_n"][0, s0:s0 + NSS]),
            "sm": f(inputs["state_mlstm_m"][0, s0:s0 + NSS]),
            "sS": f(inputs["state_gdn_S"][0, s0:s0 + NSS]),
            "scv": f(inputs["state_gdn_conv"][0, s0:s0 + NSS]),
            "ck": f(np.asarray(inputs["cache_mem_k"])[0, s0:s0 + NSS].reshape(NSS, NMEM, D)),
            "cv": f(np.asarray(inputs["cache_mem_v"])[0, s0:s0 + NSS].reshape(NSS, NMEM, D)),
            "memp": f(inputs["mem_prompt"][p0:p0 + NPS]),
            "w_in": f(inputs["w_in"][0]), "w_out": f(inputs["w_out"][0]),
            "wq": f(inputs["wq_x"][0]), "wk": f(inputs["wk_x"][0]), "wv": f(inputs["wv_x"][0]),
            "wo": f(inputs["wo_x"][0]), "w1": f(inputs["w_ff1"][0]), "w2": f(inputs["w_ff2"][0]),
            "nw_mix": f(inputs["norm_mix_w"][0]), "nw_x": f(inputs["norm_x_w"][0]),
            "nw_mem": f(inputs["norm_mem_w"][0]), "nw_ffn": f(inputs["norm_ffn_w"][0]),
            "nw_fin": f(inputs["norm_final_w"]),
            "b_ig": f(inputs["mlstm_igate_b"][0]), "b_fg": f(inputs["mlstm_fgate_b"][0]),
            "nw_a": f(inputs["mlstm_norm_w"][0]), "nw_b": f(inputs["gdn_norm_w"][0]),
            "convw": f(inputs["gdn_conv_w"][0]), "a_log": f(inputs["gdn_A_log"][0]), "dt_b": f(inputs["gdn_dt_bias"][0]),
        }
        maps.append(m)
    return maps


_CACHE = {}


def kernel(**inputs):
    stage = int(os.environ.get("KSTAGE", "9"))
    if stage not in _CACHE:
        _CACHE[stage] = build(stage)
    kb = _CACHE[stage]
    maps = _shard(inputs)
    res = run_bass_kernel_spmd(kb.nc, maps, core_ids=list(range(NCORES)))
    R = res.results
    cat = lambda nm: np.concatenate([np.asarray(r[nm], dtype=np.float32) for r in R], axis=0)
    y_prompt = cat("yp")
    y_sample = cat("ys")
    outs = (y_prompt, y_sample,
            cat("o_pC")[None], cat("o_pn")[None], cat("o_pm")[None], cat("o_pS")[None], cat("o_pcv")[None],
            cat("o_pmk").reshape(1, NCORES * NPS, NMEM, 4, 256), cat("o_pmv").reshape(1, NCORES * NPS, NMEM, 4, 256),
            cat("o_sC")[None], cat("o_sn")[None], cat("o_sm")[None], cat("o_sS")[None], cat("o_scv")[None])
    return outs
```
